# Optimizing a Trainium2 kernel written in Bass

```python
import math
import jax, jax.numpy as jnp
from jax import lax
import numpy as np

D_MODEL = 1024
BATCH = 16
SEQ = 2048
DEPTH = 2
DEC_BATCH = 16
DEC_SEQ = 4096
PAST_LEN = 128

HEAD_DIM = 64
ROT_DIM = HEAD_DIM // 4
ROPE_THETA = 500000.0
QBLK = 128
A_HEADS = D_MODEL // 256
A_QK = A_HEADS * 2 * HEAD_DIM
A_V = A_HEADS * 2 * HEAD_DIM
A_OUT = A_V
DIL_PAIRS = ((128, 1), (512, 4), (2048, 16))
N_DIL = len(DIL_PAIRS)
B_HEADS = D_MODEL // 128
B_QKV = N_DIL * B_HEADS * HEAD_DIM
B_OUT = B_HEADS * HEAD_DIM
BAND_BLK = 64
SSM_HEADS = D_MODEL // 64
SSM_HEAD_DIM = 64
SSM_GROUPS = 2
SSM_STATE = 128
SSM_INNER = SSM_HEADS * SSM_HEAD_DIM
SSM_CONV = 5
SSM_CHUNK = 128
C_XBC = SSM_INNER + 2 * SSM_GROUPS * SSM_STATE
C_DT = 2 * SSM_HEADS
DT_MIN = 0.001
DT_MAX = 0.1
N_BRANCH = 3
GATE_W = N_BRANCH * D_MODEL
SPLITS = (A_QK, A_QK, A_V, B_QKV, B_QKV, B_QKV, SSM_INNER, C_XBC, C_DT, GATE_W)
W_IN = sum(SPLITS)
FFN_HIDDEN = -(-8 * D_MODEL // (3 * 256)) * 256
PLE_DIM = 256
EPS = 1e-6
NEG = -1e30

kernel_name = 'hybrid_bidir_encoder_diffattn_dilated_ssd'


def rmsnorm(x, g):
    xf = x.astype(jnp.float32)
    y = xf * lax.rsqrt(jnp.mean(xf * xf, axis=-1, keepdims=True) + EPS)
    return (y * g.astype(jnp.float32)).astype(x.dtype)


def split_points():
    pts, acc = [], 0
    for s in SPLITS[:-1]:
        acc += s
        pts.append(acc)
    return pts


def rope_tables(seq):
    inv = ROPE_THETA ** (-jnp.arange(0, ROT_DIM, 2, dtype=jnp.float32) / ROT_DIM)
    ang = jnp.arange(seq, dtype=jnp.float32)[:, None] * inv[None, :]
    return jnp.cos(ang), jnp.sin(ang)


def apply_rope(x, cos, sin):
    half = ROT_DIM // 2
    xr = x[..., :ROT_DIM].astype(jnp.float32)
    x1, x2 = xr[..., :half], xr[..., half:]
    c = cos[:, None, :]
    s = sin[:, None, :]
    rot = jnp.concatenate([x1 * c - x2 * s, x2 * c + x1 * s], axis=-1).astype(x.dtype)
    return jnp.concatenate([rot, x[..., ROT_DIM:]], axis=-1)


def diff_attention(q, k, v, lam):
    bsz, seq, nh, _, dh = q.shape
    nb = seq // QBLK
    qb = jnp.moveaxis(q.reshape(bsz, nb, QBLK, nh, 2, dh), 1, 0)
    scale = dh ** -0.5

    def one_block(qi):
        s = jnp.einsum('bqhcd,bkhcd->bhcqk', qi, k).astype(jnp.float32) * scale
        pr = jax.nn.softmax(s, axis=-1)
        a = pr[:, :, 0] - lam * pr[:, :, 1]
        return jnp.einsum('bhqk,bkhe->bqhe', a.astype(v.dtype), v)

    o = lax.map(one_block, qb)
    return jnp.moveaxis(o, 0, 1).reshape(bsz, seq, nh, 2 * dh)


def banded_attention(q, k, v, radius):
    n, L, nh, dh = q.shape
    blk = BAND_BLK
    nb = -(-L // blk)
    lp = nb * blk
    ns = -(-radius // blk)
    qp = jnp.pad(q, ((0, 0), (0, lp - L), (0, 0), (0, 0))).reshape(n, nb, blk, nh, dh)
    padk = ((0, 0), (ns * blk, lp - L + ns * blk), (0, 0), (0, 0))
    kp = jnp.pad(k, padk).reshape(n, nb + 2 * ns, blk, nh, dh)
    vp = jnp.pad(v, padk).reshape(n, nb + 2 * ns, blk, nh, dh)
    kband = jnp.concatenate([kp[:, j:j + nb] for j in range(2 * ns + 1)], axis=2)
    vband = jnp.concatenate([vp[:, j:j + nb] for j in range(2 * ns + 1)], axis=2)
    kw = (2 * ns + 1) * blk
    s = jnp.einsum('nbqhd,nbkhd->nbhqk', qp, kband).astype(jnp.float32) * (dh ** -0.5)
    qpos = jnp.arange(nb)[:, None] * blk + jnp.arange(blk)[None, :]
    kpos = jnp.arange(nb)[:, None] * blk - ns * blk + jnp.arange(kw)[None, :]
    rel = kpos[:, None, :] - qpos[:, :, None]
    valid = (jnp.abs(rel) <= radius) & (kpos[:, None, :] >= 0) & (kpos[:, None, :] < L)
    s = jnp.where(valid[None, :, None], s, NEG)
    m = jnp.max(s, axis=-1, keepdims=True)
    e = jnp.exp(s - m)
    den = jnp.sum(e, axis=-1, keepdims=True)
    lse = (m + jnp.log(den))[..., 0]
    o = jnp.einsum('nbhqk,nbkhd->nbqhd', (e / den).astype(v.dtype), vband)
    o = o.reshape(n, lp, nh, dh)[:, :L]
    lse = lse.transpose(0, 1, 3, 2).reshape(n, lp, nh)[:, :L]
    return o, lse


def dilated_group(q, k, v, dil, radius):
    bsz, seq, nh, dh = q.shape
    L = seq // dil

    def sub(t):
        return t.reshape(bsz, L, dil, nh, dh).transpose(0, 2, 1, 3, 4).reshape(bsz * dil, L, nh, dh)

    o, lse = banded_attention(sub(q), sub(k), sub(v), radius)
    o = o.reshape(bsz, dil, L, nh, dh).transpose(0, 2, 1, 3, 4).reshape(bsz, seq, nh, dh)
    lse = lse.reshape(bsz, dil, L, nh).transpose(0, 2, 1, 3).reshape(bsz, seq, nh)
    return o, lse


def segsum(a):
    t = a.shape[-1]
    cs = jnp.cumsum(a, axis=-1)
    seg = cs[..., :, None] - cs[..., None, :]
    mask = jnp.tril(jnp.ones((t, t), dtype=bool))
    return jnp.where(mask, seg, -jnp.inf)


def ssd_chunked(xdt, adt, bm, cm):
    b, seq, g, e, p = xdt.shape
    n = bm.shape[-1]
    c = seq // SSM_CHUNK
    X = xdt.reshape(b, c, SSM_CHUNK, g, e, p)
    Bc = bm.reshape(b, c, SSM_CHUNK, g, n)
    Cc = cm.reshape(b, c, SSM_CHUNK, g, n)
    A = adt.reshape(b, c, SSM_CHUNK, g, e).transpose(0, 3, 4, 1, 2)
    acs = jnp.cumsum(A, axis=-1)
    lmat = jnp.exp(segsum(A))
    cb = jnp.einsum('bclgn,bcsgn->bgcls', Cc, Bc)
    y_diag = jnp.einsum('bgecls,bcsgep->bclgep', cb[:, :, None] * lmat, X)
    decay_states = jnp.exp(acs[..., -1:] - acs)
    states = jnp.einsum('bclgn,bgecl,bclgep->bcgepn', Bc, decay_states, X)
    chunk_decay = jnp.exp(acs[..., -1])

    def step(h, inp):
        st, dec = inp
        return h * dec[..., None, None] + st, h

    h0 = jnp.zeros((b, g, e, p, n), jnp.float32)
    _, h_in = lax.scan(step, h0, (jnp.moveaxis(states, 1, 0), jnp.moveaxis(chunk_decay, 3, 0)))
    h_in = jnp.moveaxis(h_in, 0, 1)
    y_off = jnp.einsum('bclgn,bcgepn,bgecl->bclgep', Cc, h_in, jnp.exp(acs))
    return (y_diag + y_off).reshape(b, seq, g, e, p)


def depthwise_conv(x, w, bias):
    ch = x.shape[-1]
    pad = (SSM_CONV - 1) // 2
    y = lax.conv_general_dilated(x, w[:, None, :].astype(x.dtype), (1,), [(pad, pad)],
                                 dimension_numbers=('NWC', 'WIO', 'NWC'), feature_group_count=ch)
    return y + bias.astype(x.dtype)


def mamba_mixer(z, xbc, dt_raw, conv_w, conv_b, dt_bias, a_log, ssm_d, ssm_norm_g):
    bsz, seq, _ = xbc.shape
    hpg = SSM_HEADS // SSM_GROUPS
    f32 = jnp.float32
    xbc = jax.nn.silu(depthwise_conv(xbc, conv_w, conv_b))
    xs, bm, cm = jnp.split(xbc, [SSM_INNER, SSM_INNER + SSM_GROUPS * SSM_STATE], axis=-1)
    xs = xs.astype(f32).reshape(bsz, seq, SSM_GROUPS, hpg, SSM_HEAD_DIM)
    bm = bm.astype(f32).reshape(bsz, seq, SSM_GROUPS, SSM_STATE)
    cm = cm.astype(f32).reshape(bsz, seq, SSM_GROUPS, SSM_STATE)
    dt = jax.nn.softplus(dt_raw.astype(f32).reshape(bsz, seq, 2, SSM_HEADS) + dt_bias.astype(f32))
    a = -jnp.exp(a_log.astype(f32))
    dt_f = dt[:, :, 0].reshape(bsz, seq, SSM_GROUPS, hpg)
    dt_b = dt[:, :, 1].reshape(bsz, seq, SSM_GROUPS, hpg)
    a_f = a[0].reshape(SSM_GROUPS, hpg)
    a_b = a[1].reshape(SSM_GROUPS, hpg)
    y_f = ssd_chunked(xs * dt_f[..., None], dt_f * a_f, bm, cm)
    flip = lambda t: jnp.flip(t, axis=1)
    y_b = flip(ssd_chunked(flip(xs * dt_b[..., None]), flip(dt_b * a_b), flip(bm), flip(cm)))
    y = y_f + y_b + ssm_d.astype(f32).reshape(SSM_GROUPS, hpg)[:, :, None] * xs
    y = y.reshape(bsz, seq, SSM_INNER) * jax.nn.silu(z.astype(f32))
    gsz = SSM_INNER // SSM_GROUPS
    y = rmsnorm(y.reshape(bsz, seq, SSM_GROUPS, gsz), ssm_norm_g.reshape(SSM_GROUPS, gsz))
    return y.reshape(bsz, seq, SSM_INNER).astype(z.dtype)


def encoder_layer(x, p_i, li, norm_mix_g, w_in, diff_lambda, diff_subln_g, conv_w, conv_b, dt_bias,
                  a_log, ssm_d, ssm_norm_g, w_br_a, w_br_b, w_br_c, w_out, norm_ffn_g, w_ffn_in,
                  w_ffn_out, norm_ple_g, w_ple_gate, w_ple_proj):
    bsz, seq, _ = x.shape
    f32 = jnp.float32
    cos, sin = rope_tables(seq)
    h = rmsnorm(x, norm_mix_g)
    u = h @ w_in
    a_q, a_k, a_v, b_q, b_k, b_v, c_z, c_xbc, c_dt, g_raw = jnp.split(u, split_points(), axis=-1)

    aq = apply_rope(a_q.reshape(bsz, seq, 2 * A_HEADS, HEAD_DIM), cos, sin).reshape(bsz, seq, A_HEADS, 2, HEAD_DIM)
    ak = apply_rope(a_k.reshape(bsz, seq, 2 * A_HEADS, HEAD_DIM), cos, sin).reshape(bsz, seq, A_HEADS, 2, HEAD_DIM)
    av = a_v.reshape(bsz, seq, A_HEADS, 2 * HEAD_DIM)
    lam_init = 0.8 - 0.6 * math.exp(-0.3 * li)
    lp = diff_lambda.astype(f32)
    lam = jnp.exp(jnp.sum(lp[0] * lp[1])) - jnp.exp(jnp.sum(lp[2] * lp[3])) + lam_init
    oa = diff_attention(aq, ak, av, lam)
    oa = (rmsnorm(oa, diff_subln_g) * (1.0 - lam_init)).reshape(bsz, seq, A_OUT)

    bq = apply_rope(b_q.reshape(bsz, seq, N_DIL * B_HEADS, HEAD_DIM), cos, sin).reshape(bsz, seq, N_DIL, B_HEADS, HEAD_DIM)
    bk = apply_rope(b_k.reshape(bsz, seq, N_DIL * B_HEADS, HEAD_DIM), cos, sin).reshape(bsz, seq, N_DIL, B_HEADS, HEAD_DIM)
    bv = b_v.reshape(bsz, seq, N_DIL, B_HEADS, HEAD_DIM)
    outs, lses = [], []
    for gi, (win, dil) in enumerate(DIL_PAIRS):
        o_g, l_g = dilated_group(bq[:, :, gi], bk[:, :, gi], bv[:, :, gi], dil, win // (2 * dil))
        outs.append(o_g)
        lses.append(l_g)
    wts = jax.nn.softmax(jnp.stack(lses), axis=0)
    ob = jnp.sum(wts[..., None] * jnp.stack(outs).astype(f32), axis=0).astype(x.dtype).reshape(bsz, seq, B_OUT)

    oc = mamba_mixer(c_z, c_xbc, c_dt, conv_w, conv_b, dt_bias, a_log, ssm_d, ssm_norm_g)

    gates = jax.nn.sigmoid(g_raw.astype(f32)).astype(x.dtype).reshape(bsz, seq, N_BRANCH, D_MODEL)
    m = gates[:, :, 0] * (oa @ w_br_a) + gates[:, :, 1] * (ob @ w_br_b) + gates[:, :, 2] * (oc @ w_br_c)
    x = x + m @ w_out

    gt, up = jnp.split(rmsnorm(x, norm_ffn_g) @ w_ffn_in, 2, axis=-1)
    x = x + (jax.nn.silu(gt) * up) @ w_ffn_out

    pg = jax.nn.sigmoid(rmsnorm(x, norm_ple_g) @ w_ple_gate)
    x = x + pg * (p_i @ w_ple_proj)
    return x


def trunk(x, p, norm_mix_g, w_in, diff_lambda, diff_subln_g, conv_w, conv_b, dt_bias, a_log, ssm_d,
          ssm_norm_g, w_br_a, w_br_b, w_br_c, w_out, norm_ffn_g, w_ffn_in, w_ffn_out, norm_ple_g,
          w_ple_gate, w_ple_proj, final_norm_g):
    for li in range(DEPTH):
        x = encoder_layer(x, p[li], li, norm_mix_g[li], w_in[li], diff_lambda[li], diff_subln_g[li],
                          conv_w[li], conv_b[li], dt_bias[li], a_log[li], ssm_d[li], ssm_norm_g[li],
                          w_br_a[li], w_br_b[li], w_br_c[li], w_out[li], norm_ffn_g[li], w_ffn_in[li],
                          w_ffn_out[li], norm_ple_g[li], w_ple_gate[li], w_ple_proj[li])
    return rmsnorm(x, final_norm_g)


def setup_inputs(seed: int = 0) -> dict:
    key = jax.random.key(seed)
    ks = jax.random.split(key, 32)
    f32 = jnp.float32

    def nrm(k, shape, fan_in):
        return jax.random.normal(k, shape, f32) * (fan_in ** -0.5)

    def gain(k, shape):
        return 1.0 + 0.02 * jax.random.normal(k, shape, f32)

    dt0 = jnp.exp(jax.random.uniform(ks[10], (DEPTH, 2, SSM_HEADS), f32)
                  * (math.log(DT_MAX) - math.log(DT_MIN)) + math.log(DT_MIN))
    return {
        'x_prompt': jax.random.normal(ks[0], (BATCH, SEQ, D_MODEL), f32),
        'x_sample': jax.random.normal(ks[1], (DEC_BATCH, DEC_SEQ, D_MODEL), f32),
        'p_prompt': jax.random.normal(ks[2], (DEPTH, BATCH, SEQ, PLE_DIM), f32),
        'p_sample': jax.random.normal(ks[3], (DEPTH, DEC_BATCH, DEC_SEQ, PLE_DIM), f32),
        'norm_mix_g': gain(ks[4], (DEPTH, D_MODEL)),
        'w_in': nrm(ks[5], (DEPTH, D_MODEL, W_IN), D_MODEL),
        'diff_lambda': 0.1 * jax.random.normal(ks[6], (DEPTH, 4, HEAD_DIM), f32),
        'diff_subln_g': gain(ks[7], (DEPTH, 2 * HEAD_DIM)),
        'conv_w': nrm(ks[8], (DEPTH, SSM_CONV, C_XBC), SSM_CONV),
        'conv_b': 0.02 * jax.random.normal(ks[9], (DEPTH, C_XBC), f32),
        'dt_bias': dt0 + jnp.log(-jnp.expm1(-dt0)),
        'a_log': jnp.log(jax.random.uniform(ks[11], (DEPTH, 2, SSM_HEADS), f32, minval=1.0, maxval=16.0)),
        'ssm_d': gain(ks[12], (DEPTH, SSM_HEADS)),
        'ssm_norm_g': gain(ks[13], (DEPTH, SSM_INNER)),
        'w_br_a': nrm(ks[14], (DEPTH, A_OUT, D_MODEL), A_OUT),
        'w_br_b': nrm(ks[15], (DEPTH, B_OUT, D_MODEL), B_OUT),
        'w_br_c': nrm(ks[16], (DEPTH, SSM_INNER, D_MODEL), SSM_INNER),
        'w_out': nrm(ks[17], (DEPTH, D_MODEL, D_MODEL), D_MODEL),
        'norm_ffn_g': gain(ks[18], (DEPTH, D_MODEL)),
        'w_ffn_in': nrm(ks[19], (DEPTH, D_MODEL, 2 * FFN_HIDDEN), D_MODEL),
        'w_ffn_out': nrm(ks[20], (DEPTH, FFN_HIDDEN, D_MODEL), FFN_HIDDEN),
        'norm_ple_g': gain(ks[21], (DEPTH, D_MODEL)),
        'w_ple_gate': nrm(ks[22], (DEPTH, D_MODEL, D_MODEL), D_MODEL),
        'w_ple_proj': nrm(ks[23], (DEPTH, PLE_DIM, D_MODEL), PLE_DIM),
        'final_norm_g': gain(ks[24], (D_MODEL,)),
    }


def reference(x_prompt, x_sample, p_prompt, p_sample, norm_mix_g, w_in, diff_lambda, diff_subln_g,
              conv_w, conv_b, dt_bias, a_log, ssm_d, ssm_norm_g, w_br_a, w_br_b, w_br_c, w_out,
              norm_ffn_g, w_ffn_in, w_ffn_out, norm_ple_g, w_ple_gate, w_ple_proj, final_norm_g):
    y_prompt = trunk(x_prompt, p_prompt, norm_mix_g, w_in, diff_lambda, diff_subln_g, conv_w, conv_b,
                     dt_bias, a_log, ssm_d, ssm_norm_g, w_br_a, w_br_b, w_br_c, w_out, norm_ffn_g,
                     w_ffn_in, w_ffn_out, norm_ple_g, w_ple_gate, w_ple_proj, final_norm_g)
    y_sample = trunk(x_sample, p_sample, norm_mix_g, w_in, diff_lambda, diff_subln_g, conv_w, conv_b,
                     dt_bias, a_log, ssm_d, ssm_norm_g, w_br_a, w_br_b, w_br_c, w_out, norm_ffn_g,
                     w_ffn_in, w_ffn_out, norm_ple_g, w_ple_gate, w_ple_proj, final_norm_g)
    return (y_prompt, y_sample)
```

```python
import contextlib, math, os
import numpy as np
import concourse.bass as bass
import concourse.mybir as mybir
from concourse.bass_utils import run_bass_kernel_spmd

F32 = mybir.dt.float32
BF16 = mybir.dt.bfloat16
AF = mybir.ActivationFunctionType
ALU = mybir.AluOpType

D = 1024
WIN = 11808
FFN = 2816
DEPTH = 2
EPS = 1e-6
OFF_AQ, OFF_AK, OFF_AV, OFF_BQ, OFF_BK, OFF_BV, OFF_Z, OFF_XBC, OFF_DT, OFF_G = 0, 512, 1024, 1536, 3072, 4608, 6144, 7168, 8704, 8736
DILS = (1, 4, 16)
NEGBIG = -30000.0
SMAX = 4096

class Buf:
    __slots__ = ("name", "lw", "rd", "excl")
    def __init__(self, name, excl=False):
        self.name = name
        self.lw = None
        self.rd = {}
        self.excl = excl


class Em:
    ENG = ("pe", "act", "dve", "pool", "sp")
    NDMA = 24

    def __init__(self, nc):
        self.nc = nc
        self.lists = {e: [] for e in self.ENG}
        self.cnt = {e: 0 for e in self.ENG}
        self.seen = {e: {} for e in self.ENG}
        self.dma_i = {"sp": 0, "act": 0, "pool": 0}
        self.dma_used = set()
        self.nins = 0
        self.bufs = {}

    def buf(self, key):
        b = self.bufs.get(key)
        if b is None:
            b = Buf(key)
            self.bufs[key] = b
        return b

    def _need(self, eng, deps):
        waits = []
        seen = self.seen[eng]
        for k, v in deps.items():
            if eng == "pe" and k == ("c", "pe"):
                continue
            if seen.get(k, 0) >= v:
                continue
            seen[k] = v
            waits.append((k, v))
        return waits

    def _deps(self, reads, writes):
        deps = {}
        for b in reads:
            if b.lw is not None:
                k, v = b.lw
                if deps.get(k, 0) < v:
                    deps[k] = v
            if b.excl:
                for k, v in b.rd.items():
                    if deps.get(k, 0) < v:
                        deps[k] = v
        for b in writes:
            if b.lw is not None:
                k, v = b.lw
                if deps.get(k, 0) < v:
                    deps[k] = v
            for k, v in b.rd.items():
                if deps.get(k, 0) < v:
                    deps[k] = v
        return deps

    def _mark(self, key, val, reads, writes):
        for b in reads:
            if b.rd.get(key, 0) < val:
                b.rd[key] = val
        for b in writes:
            b.lw = (key, val)
            b.rd = {}

    def op(self, eng, fn, reads=(), writes=()):
        deps = self._deps(reads, writes)
        waits = self._need(eng, deps)
        self.cnt[eng] += 1
        val = self.cnt[eng]
        key = ("c", eng)
        self.lists[eng].append((waits, fn, key, 1))
        self._mark(key, val, reads, writes)
        self.nins += 1

    def dma(self, q, fn, reads=(), writes=()):
        deps = self._deps(reads, writes)
        i = self.dma_i[q]
        self.dma_i[q] += 1
        j = i % self.NDMA
        key = ("d", q, j)
        self.dma_used.add(key)
        val = 16 * (i // self.NDMA + 1)
        if val > 16:
            deps[key] = max(deps.get(key, 0), val - 16)
        waits = self._need(q, deps)
        self.lists[q].append((waits, fn, key, 16))
        self._mark(key, val, reads, writes)
        self.nins += 1

    def barrier(self):
        allk = {}
        for e in self.ENG:
            if self.cnt[e] > 0:
                allk[("c", e)] = self.cnt[e]
        for q, n in self.dma_i.items():
            for j in range(min(n, self.NDMA)):
                allk[("d", q, j)] = 16 * ((n - j + self.NDMA - 1) // self.NDMA)
        for e in self.ENG:
            waits = self._need(e, dict(allk))
            if waits:
                self.lists[e].append((waits, None, None, 0))

    def replay(self):
        nc = self.nc
        with contextlib.ExitStack() as st:
            sems = {}
            for e in self.ENG:
                sems[("c", e)] = st.enter_context(nc.semaphore("c_" + e))
            for k in sorted(self.dma_used):
                sems[k] = st.enter_context(nc.semaphore("d_%s_%d" % (k[1], k[2])))
            block = st.enter_context(nc.Block())

            def run(lst):
                def f(eng):
                    for waits, fn, key, inc in lst:
                        for k, v in waits:
                            eng.wait_ge(sems[k], v)
                        if fn is not None:
                            fn(eng).then_inc(sems[key], inc)
                return f
            block.tensor(run(self.lists["pe"]))
            block.scalar(run(self.lists["act"]))
            block.vector(run(self.lists["dve"]))
            block.gpsimd(run(self.lists["pool"]))
            block.sync(run(self.lists["sp"]))


def make_consts():
    j = np.arange(128)[:, None]
    i = np.arange(128)[None, :]
    blocks = []
    blocks.append((j == i).astype(np.float32))
    pm = np.arange(128) % 64
    partner = np.where(pm < 8, np.arange(128) + 8, np.where(pm < 16, np.arange(128) - 8, -1))
    Rm = np.zeros((128, 128), np.float32)
    for pp in range(128):
        if partner[pp] >= 0:
            Rm[partner[pp], pp] = 1.0
    blocks.append(Rm)
    negA = np.where(j >= i, 0.0, NEGBIG).astype(np.float32)
    blocks.append(negA)
    blocks.append(np.where(j < 64, NEGBIG, negA).astype(np.float32))
    negB = np.where(j <= i, 0.0, NEGBIG).astype(np.float32)
    blocks.append(negB)
    blocks.append(np.where(j >= 64, NEGBIG, negB).astype(np.float32))
    blocks.append((j <= i).astype(np.float32))
    blocks.append((j >= i).astype(np.float32))
    blocks.append(np.broadcast_to((j == 127), (128, 128)).astype(np.float32))
    blocks.append(np.broadcast_to((j == 0), (128, 128)).astype(np.float32))
    blocks.append(np.where(j <= i, 0.0, NEGBIG).astype(np.float32))
    blocks.append(np.where(j >= i, 0.0, NEGBIG).astype(np.float32))
    blocks.append(np.ones((128, 128), np.float32))
    cpack = np.concatenate(blocks, axis=1)
    inv = (500000.0 ** (-np.arange(0, 16, 2, dtype=np.float32) / 16)).astype(np.float32)
    ang = np.arange(SMAX, dtype=np.float32)[None, :] * inv[:, None]
    cosT = np.ones((128, SMAX), np.float32)
    sinT = np.zeros((128, SMAX), np.float32)
    for pp in range(128):
        m = pp % 64
        if m < 8:
            cosT[pp] = np.cos(ang[m]); sinT[pp] = -np.sin(ang[m])
        elif m < 16:
            cosT[pp] = np.cos(ang[m - 8]); sinT[pp] = np.sin(ang[m - 8])
    sel32 = np.zeros((32, 32, 128), np.float32)
    for k in range(32):
        sel32[k, k, :] = 1.0
    return cpack, cosT, sinT, sel32.reshape(32, 32 * 128)


NCB = 13


def lam_init(li):
    return 0.8 - 0.6 * math.exp(-0.3 * li)


SP_GMIX, SP_GFFN, SP_GPLE, SP_GSSM, SP_CONVW, SP_CONVB, SP_GSUB, SP_PER_LAYER = 0, 8, 16, 24, 32, 92, 104, 105
RP_LAM, RP_DTB, RP_ALOG, RP_SSMD, RP_PER_LAYER = 0, 256, 288, 320, 336
RP_FIN = RP_PER_LAYER * DEPTH
RP_TOT = RP_FIN + D


def make_packs(inp):
    sp = np.zeros((128, SP_PER_LAYER * DEPTH), np.float32)
    rp = np.zeros((1, RP_TOT), np.float32)
    for l in range(DEPTH):
        o = l * SP_PER_LAYER
        sp[:, o + SP_GMIX:o + SP_GMIX + 8] = inp["norm_mix_g"][l].reshape(8, 128).T
        sp[:, o + SP_GFFN:o + SP_GFFN + 8] = inp["norm_ffn_g"][l].reshape(8, 128).T
        sp[:, o + SP_GPLE:o + SP_GPLE + 8] = inp["norm_ple_g"][l].reshape(8, 128).T
        sp[:, o + SP_GSSM:o + SP_GSSM + 8] = inp["ssm_norm_g"][l].reshape(8, 128).T
        cw = inp["conv_w"][l].reshape(5, 12, 128)
        sp[:, o + SP_CONVW:o + SP_CONVW + 60] = cw.transpose(2, 1, 0).reshape(128, 60)
        sp[:, o + SP_CONVB:o + SP_CONVB + 12] = inp["conv_b"][l].reshape(12, 128).T
        sp[:, o + SP_GSUB] = inp["diff_subln_g"][l]
        r = l * RP_PER_LAYER
        rp[0, r + RP_LAM:r + RP_LAM + 256] = inp["diff_lambda"][l].reshape(-1)
        rp[0, r + RP_DTB:r + RP_DTB + 32] = inp["dt_bias"][l].reshape(-1)
        rp[0, r + RP_ALOG:r + RP_ALOG + 32] = inp["a_log"][l].reshape(-1)
        rp[0, r + RP_SSMD:r + RP_SSMD + 16] = inp["ssm_d"][l].reshape(-1)
    rp[0, RP_FIN:] = inp["final_norm_g"]
    return sp, rp


WEIGHTS = (("w_in", D, WIN), ("w_br_a", 512, D), ("w_br_b", 512, D), ("w_br_c", D, D), ("w_out", D, D),
           ("w_ffn_in", D, 2 * FFN), ("w_ffn_out", FFN, D), ("w_ple_gate", D, D), ("w_ple_proj", 256, D))


class Builder:
    AW = 47 * 1024

    def __init__(self, seqs, dbg=False, stages=None, nlayers=DEPTH):
        self.nlayers = nlayers
        self.seqs = list(seqs)
        self.TOK = sum(seqs)
        self.dbg = dbg
        self.stages = stages
        self.nc = nc = bass.Bass("TRN2", target_bir_lowering=False)
        self.em = Em(nc)
        TOK = self.TOK
        skind = "ExternalOutput" if dbg else "Internal"

        def dram(name, shape, dt, kind):
            return nc.dram_tensor(name, list(shape), dt, kind=kind).ap()
        self.x = dram("x", [TOK, D], F32, "ExternalInput")
        self.p = dram("p", [DEPTH, TOK, 256], F32, "ExternalInput")
        self.cpack = dram("cpack", [128, NCB * 128], F32, "ExternalInput")
        self.cosd = dram("cosT", [128, SMAX], F32, "ExternalInput")
        self.sind = dram("sinT", [128, SMAX], F32, "ExternalInput")
        self.sel32d = dram("sel32", [32, 32 * 128], F32, "ExternalInput")
        self.spd = dram("spack", [128, SP_PER_LAYER * DEPTH], F32, "ExternalInput")
        self.rpd = dram("rpack", [1, RP_TOT], F32, "ExternalInput")
        self.w = {}
        self.wb = {}
        for name, k, n in WEIGHTS:
            self.w[name] = dram(name, [DEPTH, k, n], F32, "ExternalInput")
            self.wb[name] = dram("b_" + name, [DEPTH, k, n], BF16, "Internal")
        self.y = dram("y", [TOK, D], F32, "ExternalOutput")
        self.xs1 = dram("xs1", [TOK, D], F32, skind)
        self.oT = dram("oT", [2048, TOK], BF16, skind)
        self.gT = dram("gT", [3072, TOK], BF16, skind)
        self.bpart = dram("bpart", [3, TOK, 520], F32, skind)
        self.xstok = dram("xstok", [TOK, 1024], BF16, skind)
        self.btok = dram("btok", [TOK, 256], BF16, skind)
        self.bct = dram("bct", [512, TOK], BF16, skind)
        self.szd = dram("szd", [TOK, 1024], BF16, skind)
        self.hbd = dram("hbd", [2, TOK // 128, 128, 512], BF16, skind)

    def alloc(self, n_free, dt, parts=128):
        words = n_free if dt == F32 else (n_free + 1) // 2
        assert self.top + words <= self.AW, ("arena overflow", self.top, words)
        v = self.arena[0:parts, self.top:self.top + words]
        self.top += words
        if dt != F32:
            v = v.bitcast(dt)[:, 0:n_free]
        return v

    def alloc3(self, a, b, dt):
        return self.alloc(a * b, dt).rearrange("p (a b) -> p a b", a=a)

    _uid = 0

    def nb(self, tag="b"):
        Builder._uid += 1
        return Buf((tag, Builder._uid))

    def ring(self, n, n_free, dt, tag="r"):
        return [(self.alloc(n_free, dt), self.nb(tag)) for _ in range(n)]

    def db(self, *key):
        return self.em.buf(key)

    def wslab(self, name, l, c0, ncols, kc_n, r0=0):
        return self.wb[name][l, r0:r0 + kc_n * 128, c0:c0 + ncols].rearrange("(kc p) n -> p kc n", p=128)

    def build(self):
        nc, em = self.nc, self.em
        with contextlib.ExitStack() as st:
            self.arena = st.enter_context(nc.sbuf_tensor("arena", [128, self.AW], F32))
            self.top = 0
            self.ps = []
            self.psb = []
            for i in range(8):
                t = st.enter_context(nc.psum_tensor("ps%d" % i, [128, 512], F32))
                self.ps.append(t[:, :])
                self.psb.append(Buf(("ps", i), excl=True))
            self.setup()
            tok0 = 0
            for si, S in enumerate(self.seqs):
                self.run_seq(si, tok0, S)
                tok0 += S
            em.barrier()
            em.replay()
        return nc

    def setup(self):
        em, nc = self.em, self.nc
        self.cpf = self.alloc(NCB * 128, F32)
        self.cpb = self.alloc(NCB * 128, BF16)
        self.cosb = self.alloc(SMAX, BF16)
        self.sinb = self.alloc(SMAX, BF16)
        self.spf = self.alloc(SP_PER_LAYER * DEPTH, F32)
        self.rpf = self.alloc(RP_TOT, F32)
        self.neglam = self.alloc(DEPTH, F32)
        self.gsub = self.alloc(DEPTH, F32)
        self.aneg = self.alloc(32 * DEPTH, F32)
        cb = self.nb("const")
        em.dma("sp", lambda e: e.dma_start(self.cpf, self.cpack[:, :]), writes=[cb])
        em.dma("sp", lambda e: e.dma_start(self.spf, self.spd[:, :]), writes=[cb])
        em.dma("sp", lambda e: e.dma_start(self.rpf, self.rpd[0:1, :].broadcast_to([128, RP_TOT])), writes=[cb])
        em.dma("pool", lambda e: e.dma_start(self.cosb, self.cosd[:, :]), writes=[cb])
        em.dma("pool", lambda e: e.dma_start(self.sinb, self.sind[:, :]), writes=[cb])
        em.op("dve", lambda e: e.tensor_copy(self.cpb, self.cpf), reads=[cb], writes=[cb])
        import os
        SK = os.environ.get('SKIP', '')
        for name, k, n in (WEIGHTS if 'p' not in SK else ()):
            for l in range(DEPTH):
                for r in range(0, k, 128):
                    rr = min(128, k - r)
                    em.dma("pool", lambda e, name=name, l=l, r=r, rr=rr: e.dma_start(self.wb[name][l, r:r + rr, :], self.w[name][l, r:r + rr, :]))
        tmp = self.alloc(256, F32)
        t2 = self.alloc(4, F32)
        for l in (range(DEPTH) if 's' not in SK else ()):
            r = l * RP_PER_LAYER
            lp = self.rpf[:, r + RP_LAM:r + RP_LAM + 256].rearrange("p (a d) -> p a d", a=4)
            pr = tmp[:, 0:128].rearrange("p (a d) -> p a d", a=2)
            em.op("dve", lambda e, lp=lp, pr=pr: e.tensor_tensor(pr, lp[:, 0:4:2, :], lp[:, 1:4:2, :], ALU.mult), reads=[cb], writes=[cb])
            em.op("dve", lambda e, pr=pr: e.tensor_reduce(t2[:, 0:2], pr, mybir.AxisListType.X, ALU.add), reads=[cb], writes=[cb])
            em.op("act", lambda e: e.activation(t2[:, 2:4], t2[:, 0:2], AF.Exp), reads=[cb], writes=[cb])
            em.op("dve", lambda e, l=l: e.scalar_tensor_tensor(self.neglam[:, l:l + 1], t2[:, 3:4], -lam_init(l), t2[:, 2:3], ALU.add, ALU.subtract), reads=[cb], writes=[cb])
            em.op("dve", lambda e, l=l: e.tensor_scalar(self.gsub[:, l:l + 1], self.spf[:, l * SP_PER_LAYER + SP_GSUB:l * SP_PER_LAYER + SP_GSUB + 1], 1.0 - lam_init(l), None, ALU.mult), reads=[cb], writes=[cb])
            em.op("act", lambda e, l=l, r=r: e.activation(self.aneg[:, l * 32:(l + 1) * 32], self.rpf[:, r + RP_ALOG:r + RP_ALOG + 32], AF.Exp), reads=[cb], writes=[cb])
            em.op("dve", lambda e, l=l: e.tensor_scalar(self.aneg[:, l * 32:(l + 1) * 32], self.aneg[:, l * 32:(l + 1) * 32], -1.0, None, ALU.mult), reads=[cb], writes=[cb])
        self.cb = cb
        self.n_junk = self.alloc(D, BF16)
        self.n_xn = self.ring(2, D, BF16, "xn")
        self.n_st = self.ring(2, 4, F32, "nst")
        self.n_i = 0
        self.xt = self.ring(2, D, F32, "xt")
        em.barrier()
        self.base_top = self.top

    def cblk(self, i, bf=True):
        t = self.cpb if bf else self.cpf
        return t[:, i * 128:(i + 1) * 128]

    def norm_tile(self, xt, xtb, gcol, dst, dstb, pbanks=(6, 7)):
        em = self.em
        i = self.n_i
        self.n_i += 1
        xn, xnb = self.n_xn[i % 2]
        stt, stb = self.n_st[i % 2]
        junk = self.n_junk
        jb = self.db("junk")
        em.op("act", lambda e: e.activation(junk, xt, AF.Square, accum_out=stt[:, 0:1]), reads=[xtb], writes=[jb, stb])
        em.op("dve", lambda e: e.tensor_scalar(stt[:, 1:2], stt[:, 0:1], 1.0 / D, EPS, ALU.mult, ALU.add), reads=[stb], writes=[stb])
        em.op("act", lambda e: e.activation(stt[:, 2:3], stt[:, 1:2], AF.Sqrt), reads=[stb], writes=[stb])
        em.op("dve", lambda e: e.reciprocal(stt[:, 3:4], stt[:, 2:3]), reads=[stb], writes=[stb])
        em.op("dve", lambda e: e.tensor_scalar(xn, xt, stt[:, 3:4], None, ALU.mult), reads=[xtb, stb], writes=[xnb])
        for half in range(2):
            bi = pbanks[half]
            pv = self.ps[bi].bitcast(BF16)[:, 0:512]
            for c in range(4):
                cc = half * 4 + c
                em.op("pe", lambda e, c=c, cc=cc, pv=pv: e.transpose(pv[:, c * 128:(c + 1) * 128], xn[:, cc * 128:(cc + 1) * 128], self.cblk(0)), reads=[xnb, self.cb], writes=[self.psb[bi]])
            g4 = gcol[:, half * 4:half * 4 + 4].unsqueeze(2).broadcast_to([128, 4, 128])
            em.op("dve", lambda e, pv=pv, half=half, g4=g4: e.tensor_tensor(dst[:, half * 4:half * 4 + 4, :], pv.rearrange("p (c n) -> p c n", c=4), g4, ALU.mult), reads=[self.psb[bi], self.cb], writes=[dstb])

    def run_seq(self, si, tok0, S):
        em = self.em
        NB = S // 512
        for l in range(self.nlayers):
            self.top = self.base_top
            st = self.stages
            NT = S // 128
            self.dt_ = self.alloc(NT * 32, F32).rearrange("p (t c) -> p t c", c=32)
            hT = self.alloc(8 * S, BF16).rearrange("p (c t) -> p c t", c=8)
            hTb = [self.nb("hT") for _ in range(NB)]
            import os
            if 'n' not in os.environ.get('SKIP', ''):
                self.stage_norm(si, tok0, S, l, hT, hTb)
            m0 = self.top
            if st is None or "A" in st:
                self.stage_A(si, tok0, S, l, hT, hTb)
                em.barrier()
            self.top = m0
            if st is None or "B" in st:
                self.stage_B(si, tok0, S, l, hT, hTb)
                em.barrier()
            self.top = m0
            if st is None or "C" in st:
                self.stage_Cprep(si, tok0, S, l, hT, hTb)
                em.barrier()
            self.top = m0
            if st is None or "G" in st:
                self.stage_G(si, tok0, S, l, hT, hTb)
                em.barrier()
            self.top = m0 - (8 * S) // 2
            m1 = self.top
            if st is None or "B" in st:
                self.stage_Bmerge(si, tok0, S, l)
                em.barrier()
            self.top = m1
            if st is None or "C" in st:
                self.stage_Cscan(si, tok0, S, l)
                em.barrier()
            self.top = self.base_top
            if st is None or "M" in st:
                self.stage_M(si, tok0, S, l)
                em.barrier()

    def stage_norm(self, si, tok0, S, l, hT, hTb):
        em = self.em
        src = self.x if l == 0 else self.xs1
        gcol = self.spf[:, l * SP_PER_LAYER + SP_GMIX:l * SP_PER_LAYER + SP_GMIX + 8]
        for tt in range(S // 128):
            xt, xtb = self.xt[tt % 2]
            t0 = tok0 + tt * 128
            rd = [self.db("xs1", t0 // 512)] if l > 0 else []
            em.dma("sp", lambda e, xt=xt, t0=t0: e.dma_start(xt, src[t0:t0 + 128, :]), reads=rd, writes=[xtb])
            self.norm_tile(xt, xtb, gcol, hT[:, :, tt * 128:(tt + 1) * 128], hTb[tt // 4])

    def rope_block(self, bi, bi2, view, dst, dstb, cview, sview):
        em = self.em
        k = self.rope_i
        self.rope_i += 1
        ub, ubb = self.r_ub[k % 2]
        t1, t1b = self.r_t1[k % 2]
        t2, t2b = self.r_t2[k % 2]
        em.op("act", lambda e: e.copy(ub, self.ps[bi]), reads=[self.psb[bi]], writes=[ubb])
        em.op("pe", lambda e: e.matmul(self.ps[bi2], self.cblk(1), ub, start=True, stop=True), reads=[ubb, self.cb], writes=[self.psb[bi2]])
        em.op("dve", lambda e: e.tensor_tensor(view(t1), view(self.ps[bi]), cview, ALU.mult), reads=[self.psb[bi], self.cb], writes=[t1b])
        em.op("dve", lambda e: e.tensor_tensor(view(t2), view(self.ps[bi2]), sview, ALU.mult), reads=[self.psb[bi2], self.cb], writes=[t2b])
        em.op(os.environ.get("ROPE_ENG", "pool"), lambda e: e.tensor_tensor(dst, view(t1), view(t2), ALU.add), reads=[t1b, t2b], writes=[dstb])

    def rope_alloc(self):
        self.r_ub = self.ring(2, 512, BF16, "ub")
        self.r_t1 = self.ring(2, 512, F32, "t1")
        self.r_t2 = self.ring(2, 512, F32, "t2")
        self.rope_i = 0

    def stage_A(self, si, tok0, S, l, hT, hTb):
        em = self.em
        NT, NB = S // 128, S // 512
        ps, psb = self.ps, self.psb
        self.rope_alloc()
        v_all = self.alloc3(NT, 128, BF16)
        vb = [self.nb("v") for _ in range(NT)]
        wvr = self.ring(2, 8 * 128, BF16, "wv")

        def vproj(hh):
            wvt, wvb = wvr[hh % 2]
            wv = wvt.rearrange("p (k n) -> p k n", k=8)
            em.dma("sp", lambda e: e.dma_start(wv, self.wslab("w_in", l, OFF_AV + hh * 128, 128, 8)), writes=[wvb])
            for tt in range(NT):
                bi = tt % 2
                for kc in range(8):
                    em.op("pe", lambda e, tt=tt, kc=kc, bi=bi: e.matmul(ps[bi][:, 0:128], hT[:, kc, tt * 128:(tt + 1) * 128], wv[:, kc, :], start=(kc == 0), stop=(kc == 7)),
                          reads=[hTb[tt // 4], wvb], writes=[psb[bi]])
                em.op("act", lambda e, tt=tt, bi=bi: e.copy(v_all[:, tt, :], ps[bi][:, 0:128]), reads=[psb[bi]], writes=[vb[tt]])
        import os
        ACUT = 9
        wqk = [(self.alloc3(8, 128, BF16), self.nb("wqk")) for _ in range(4)]
        qT = self.alloc(S, BF16)
        kT = self.alloc(S, BF16)
        qTb = [self.nb("qT") for _ in range(NB)]
        kTb = [self.nb("kT") for _ in range(NB)]
        pT = self.ring(4, 512, BF16, "pT")
        ep = self.ring(6, 512, F32, "ep")
        sq = self.ring(1, 512, BF16, "sq")[0]
        oaT = self.ring(2, 512, BF16, "oaT")
        ones_b = self.cblk(12)
        pti = 0
        for hh in range(4):
            vproj(hh)
            for wi, (off, dst, dstb) in enumerate(((OFF_AQ, qT, qTb), (OFF_AK, kT, kTb))):
                wt, wtb = wqk[(hh * 2 + wi) % 4]
                em.dma("sp", lambda e, wt=wt, off=off, hh=hh: e.dma_start(wt, self.wslab("w_in", l, off + hh * 128, 128, 8)), writes=[wtb])
                for blk in range(NB):
                    bi = blk % 2
                    for kc in range(8):
                        em.op("pe", lambda e, wt=wt, kc=kc, blk=blk, bi=bi: e.matmul(ps[bi], wt[:, kc, :], hT[:, kc, blk * 512:(blk + 1) * 512], start=(kc == 0), stop=(kc == 7)),
                              reads=[wtb, hTb[blk]], writes=[psb[bi]])
                    sl = slice(blk * 512, (blk + 1) * 512)
                    self.rope_block(bi, 2 + bi, lambda a: a, dst[:, sl], dstb[blk], self.cosb[:, sl], self.sinb[:, sl])
            for qb in range(NB):
                steps = [(kt, c) for kt in range(NT) for c in range(2)]

                def issue_score(s_, qb=qb):
                    kt, c = steps[s_]
                    sb = s_ % 4
                    hs = slice(c * 64, (c + 1) * 64)
                    em.op("pe", lambda e: e.matmul(ps[sb], kT[hs, kt * 128:(kt + 1) * 128], qT[hs, qb * 512:(qb + 1) * 512], start=True, stop=True),
                          reads=[kTb[kt // 4], qTb[qb]], writes=[psb[sb]])
                    pt, ptb = pT[s_ % 4]
                    em.op("act", lambda e: e.activation(pt, ps[sb], AF.Exp, scale=0.125), reads=[psb[sb]], writes=[ptb])

                def issue_av(s_):
                    kt, c = steps[s_]
                    pt, ptb = pT[s_ % 4]
                    em.op("pe", lambda e: e.matmul(ps[4 + 2 * c], v_all[:, kt, :], pt, start=(kt == 0), stop=(kt == NT - 1)),
                          reads=[ptb, vb[kt]], writes=[psb[4 + 2 * c]])
                    em.op("pe", lambda e: e.matmul(ps[5 + 2 * c], ones_b, pt, start=(kt == 0), stop=(kt == NT - 1)),
                          reads=[ptb, self.cb], writes=[psb[5 + 2 * c]])
                issue_score(0)
                issue_score(1)
                for kt in range(NT):
                    if kt + 1 < NT:
                        issue_score(2 * kt + 2)
                        issue_score(2 * kt + 3)
                    issue_av(2 * kt)
                    issue_av(2 * kt + 1)
                (r0, r0b), (r1, r1b), (t0, t0b), (t1, t1b), (o, ob), (rs, rsb) = ep
                em.op("dve", lambda e: e.reciprocal(r0, ps[5]), reads=[psb[5]], writes=[r0b])
                em.op("dve", lambda e: e.reciprocal(r1, ps[7]), reads=[psb[7]], writes=[r1b])
                em.op("dve", lambda e: e.tensor_tensor(t0, ps[4], r0, ALU.mult), reads=[psb[4], r0b], writes=[t0b])
                em.op("dve", lambda e: e.tensor_tensor(t1, ps[6], r1, ALU.mult), reads=[psb[6], r1b], writes=[t1b])
                em.op("dve", lambda e: e.scalar_tensor_tensor(o, t1, self.neglam[:, l:l + 1], t0, ALU.mult, ALU.add), reads=[t0b, t1b, self.cb], writes=[ob])
                em.op("act", lambda e: e.activation(sq[0], o, AF.Square), reads=[ob], writes=[sq[1]])
                em.op("pe", lambda e: e.matmul(ps[5], ones_b, sq[0], start=True, stop=True), reads=[sq[1], self.cb], writes=[psb[5]])
                em.op("dve", lambda e: e.tensor_scalar(rs, ps[5], 1.0 / 128, EPS, ALU.mult, ALU.add), reads=[psb[5]], writes=[rsb])
                em.op("act", lambda e: e.activation(rs, rs, AF.Sqrt), reads=[rsb], writes=[rsb])
                em.op("dve", lambda e: e.reciprocal(rs, rs), reads=[rsb], writes=[rsb])
                em.op("dve", lambda e: e.tensor_tensor(t0, o, rs, ALU.mult), reads=[ob, rsb], writes=[t0b])
                oa, oab = oaT[(hh * NB + qb) % 2]
                em.op("dve", lambda e, oa=oa: e.tensor_scalar(oa, t0, self.gsub[:, l:l + 1], None, ALU.mult), reads=[t0b, self.cb], writes=[oab])
                g0 = tok0 + qb * 512
                em.dma("pool", lambda e, oa=oa, hh=hh, g0=g0: e.dma_start(self.oT[hh * 128:(hh + 1) * 128, g0:g0 + 512], oa),
                       reads=[oab], writes=[self.db("oT", g0 // 512, "a%d" % hh)])

    def stage_B(self, si, tok0, S, l, hT, hTb):
        em = self.em
        NT, NB = S // 128, S // 512
        ps, psb = self.ps, self.psb
        self.rope_alloc()
        ident_b = self.cblk(0)
        wq = self.ring(2, 8 * 128, BF16, "wq")
        wk = self.ring(2, 8 * 128, BF16, "wk")
        wv = self.ring(2, 8 * 128, BF16, "wv")
        qT = self.alloc(S, BF16)
        qTb = self.nb("qTc")
        kTp = self.alloc(S + 16 * 128, BF16)
        kTb = self.nb("kTp")
        vaug = self.alloc((NT + 16) * 130, BF16)
        vab = self.nb("vaug")
        pA = self.ring(2, 512, BF16, "pA")
        pB = self.ring(2, 512, BF16, "pB")
        stg = self.ring(2, 4 * 130, F32, "stg")
        state = {'it': 0}

        def group(g, dil):
            L = S // dil
            nt_c = L // 128
            ntile = dil * (nt_c + 1)
            va = vaug[:, 0:ntile * 130].rearrange("p (t h e) -> p t h e", h=2, e=65)
            kp = kTp[:, 0:dil * (L + 128)].rearrange("p (r i) -> p r i", r=dil)
            q3 = qT[:, 0:S].rearrange("p (r i) -> p r i", r=dil)
            hTv = [hT[:, kc, :].rearrange("p (i d) -> p d i", d=dil) for kc in range(8)]
            cosv = self.cosb[:, 0:S].rearrange("p (i d) -> p d i", d=dil)
            sinv = self.sinb[:, 0:S].rearrange("p (i d) -> p d i", d=dil)
            nr = max(1, 512 // L)
            ni = min(512, L)
            view = lambda a, nr=nr: a.rearrange("p (r i) -> p r i", r=nr)

            def pair(hp):
                it = state['it']
                wqt, wqb = wq[it % 2]
                wkt, wkb = wk[it % 2]
                wvt, wvb = wv[it % 2]
                state['it'] += 1
                wq3 = wqt.rearrange("p (k n) -> p k n", k=8)
                wk3 = wkt.rearrange("p (k n) -> p k n", k=8)
                wv3 = wvt.rearrange("p (k n) -> p k n", k=8)
                cq = g * 512 + hp * 128
                em.dma("sp", lambda e, wq3=wq3, cq=cq: e.dma_start(wq3, self.wslab("w_in", l, OFF_BQ + cq, 128, 8)), writes=[wqb])
                em.dma("sp", lambda e, wk3=wk3, cq=cq: e.dma_start(wk3, self.wslab("w_in", l, OFF_BK + cq, 128, 8)), writes=[wkb])
                em.dma("sp", lambda e, wv3=wv3, cq=cq: e.dma_start(wv3, self.wslab("w_in", l, OFF_BV + cq, 128, 8)), writes=[wvb])
                em.op("pool", lambda e, kp=kp: e.memset(kp, 0.0), writes=[kTb])
                em.op("pool", lambda e, va=va: e.memset(va, 0.0), writes=[vab])
                em.op("pool", lambda e, va=va: e.memset(va[:, :, :, 64:65], 1.0), writes=[vab])
                BCUT = int(os.environ.get('BCUT', '9'))
                if BCUT < 1:
                    return
                for w3, wb_, isq in ((wq3, wqb, True), (wk3, wkb, False)):
                    for blk in range(NB):
                        r0 = (blk * 512) // L
                        i0 = (blk * 512) % L
                        bi = blk % 2
                        for kc in range(8):
                            em.op("pe", lambda e, w3=w3, kc=kc, bi=bi, r0=r0, i0=i0: e.matmul(view(ps[bi]), w3[:, kc, :], hTv[kc][:, r0:r0 + nr, i0:i0 + ni], start=(kc == 0), stop=(kc == 7)),
                                  reads=[wb_] + hTb, writes=[psb[bi]])
                        if isq:
                            dst, dstb = q3[:, r0:r0 + nr, i0:i0 + ni], qTb
                        else:
                            dst, dstb = kp[:, r0:r0 + nr, 64 + i0:64 + i0 + ni], kTb
                        self.rope_block(bi, 2 + bi, view, dst, dstb, cosv[:, r0:r0 + nr, i0:i0 + ni], sinv[:, r0:r0 + nr, i0:i0 + ni])
                if BCUT < 2:
                    return
                vi = 0
                for r in range(dil):
                    for a in range(nt_c + 1):
                        lo, hi = max(0, 128 * a - 64), min(L, 128 * a + 64)
                        p0 = lo - (128 * a - 64)
                        n = hi - lo
                        bi = 4 + vi % 2
                        vi += 1
                        for kc in range(8):
                            em.op("pe", lambda e, kc=kc, bi=bi, r=r, lo=lo, hi=hi, p0=p0, n=n: e.matmul(ps[bi][p0:p0 + n, 0:128], hTv[kc][:, r, lo:hi], wv3[:, kc, :], start=(kc == 0), stop=(kc == 7)),
                                  reads=[wvb] + hTb, writes=[psb[bi]])
                        ti = r * (nt_c + 1) + a
                        em.op("act", lambda e, bi=bi, p0=p0, n=n, ti=ti: e.copy(va[p0:p0 + n, ti, :, 0:64], ps[bi][p0:p0 + n, 0:128].rearrange("p (h e) -> p h e", h=2)),
                              reads=[psb[bi]], writes=[vab])
                if BCUT < 3:
                    return
                nq = min(4, nt_c)
                gi = 0
                for r in range(dil):
                    for qg in range(nt_c // nq):
                        sg, sgb = stg[gi % 2]
                        gi += 1
                        sg4 = sg.rearrange("p (j h e) -> p j h e", h=2, e=65)
                        for h in range(2):
                            hs = slice(h * 64, (h + 1) * 64)
                            ba, bb, bo = 0 + h, 2 + h, 6 + h
                            for jq in range(nq):
                                qi = qg * nq + jq
                                cs_ = slice(jq * 128, (jq + 1) * 128)
                                mA = self.cblk(3 if qi == 0 else 2)
                                mB = self.cblk(5 if qi == nt_c - 1 else 4)
                                em.op("pe", lambda e, ba=ba, cs_=cs_, hs=hs, r=r, qi=qi: e.matmul(ps[ba][:, cs_], kp[hs, r, 128 * qi:128 * qi + 128], q3[hs, r, 128 * qi:128 * qi + 128], start=True, stop=False),
                                      reads=[kTb, qTb], writes=[psb[ba]])
                                em.op("pe", lambda e, ba=ba, cs_=cs_, mA=mA: e.matmul(ps[ba][:, cs_], ident_b, mA, start=False, stop=True), reads=[self.cb], writes=[psb[ba]])
                                em.op("pe", lambda e, bb=bb, cs_=cs_, hs=hs, r=r, qi=qi: e.matmul(ps[bb][:, cs_], kp[hs, r, 128 * qi + 128:128 * qi + 256], q3[hs, r, 128 * qi:128 * qi + 128], start=True, stop=False),
                                      reads=[kTb, qTb], writes=[psb[bb]])
                                em.op("pe", lambda e, bb=bb, cs_=cs_, mB=mB: e.matmul(ps[bb][:, cs_], ident_b, mB, start=False, stop=True), reads=[self.cb], writes=[psb[bb]])
                            pa, pab = pA[h]
                            pb, pbb = pB[h]
                            w_ = nq * 128
                            em.op("act", lambda e, pa=pa, ba=ba, w_=w_: e.activation(pa[:, 0:w_], ps[ba][:, 0:w_], AF.Exp, scale=0.125), reads=[psb[ba]], writes=[pab])
                            em.op("act", lambda e, pb=pb, bb=bb, w_=w_: e.activation(pb[:, 0:w_], ps[bb][:, 0:w_], AF.Exp, scale=0.125), reads=[psb[bb]], writes=[pbb])
                        for h in range(2):
                            hs = slice(h * 64, (h + 1) * 64)
                            ba, bb, bo = 0 + h, 2 + h, 6 + h
                            pa, pab = pA[h]
                            pb, pbb = pB[h]
                            for jq in range(nq):
                                qi = qg * nq + jq
                                cs_ = slice(jq * 128, (jq + 1) * 128)
                                ti = r * (nt_c + 1) + qi
                                em.op("pe", lambda e, bo=bo, jq=jq, pa=pa, cs_=cs_, ti=ti, h=h: e.matmul(ps[bo][:, jq * 65:(jq + 1) * 65], pa[:, cs_], va[:, ti, h, :], start=True, stop=False),
                                      reads=[pab, vab], writes=[psb[bo]])
                                em.op("pe", lambda e, bo=bo, jq=jq, pb=pb, cs_=cs_, ti=ti, h=h: e.matmul(ps[bo][:, jq * 65:(jq + 1) * 65], pb[:, cs_], va[:, ti + 1, h, :], start=False, stop=True),
                                      reads=[pbb, vab], writes=[psb[bo]])
                            em.op("dve", lambda e, bo=bo, h=h, sg4=sg4: e.tensor_copy(sg4[:, 0:nq, h, :], ps[bo][:, 0:nq * 65].rearrange("p (j e) -> p j e", e=65)), reads=[psb[bo]], writes=[sgb])
                        base = tok0 + r + 128 * qg * nq * dil
                        dstv = self.bpart[g, base:base + (nq * 128 - 1) * dil + 1:dil, hp * 130:(hp + 1) * 130].rearrange("(j q) c -> q j c", q=128)
                        srcv = sg[:, 0:nq * 130].rearrange("p (j c) -> p j c", c=130)
                        em.dma("pool", lambda e, dstv=dstv, srcv=srcv: e.dma_start(dstv, srcv), reads=[sgb], writes=[self.db("bpart", si, l, g, hp, r, qg)])
            for hp in range(4):
                pair(hp)

        for g, dil in enumerate(DILS):
            group(g, dil)

    def stage_Bmerge(self, si, tok0, S, l):
        em = self.em
        if int(os.environ.get('BCUT', '9')) < 5:
            return
        ps, psb = self.ps, self.psb
        NB = S // 512
        part = self.ring(2, 3 * 4 * 520, F32, "part")
        acc = self.ring(2, 4 * 520, F32, "acc")
        rec = self.ring(2, 32, F32, "rec")
        obn = self.ring(2, 4 * 512, BF16, "obn")
        obT = self.ring(2, 4 * 512, BF16, "obT")
        for blk in range(NB):
            g0 = tok0 + blk * 512
            pt, ptb = part[blk % 2]
            ac, acb = acc[blk % 2]
            rc, rcb = rec[blk % 2]
            on, onb = obn[blk % 2]
            oT_, oTb_ = obT[blk % 2]
            p4 = pt.rearrange("p (g t c) -> p g t c", g=3, t=4)
            for g in range(3):
                em.dma("sp", lambda e, g=g, p4=p4, g0=g0: e.dma_start(p4[:, g, :, :], self.bpart[g, g0:g0 + 512, :].rearrange("(t q) c -> q t c", q=128)),
                       reads=[b for k, b in self.em.bufs.items() if k[0] == "bpart" and k[1] == si and k[2] == l and k[3] == g], writes=[ptb])
            MCUT = int(os.environ.get('MCUT', '9'))
            if MCUT < 2:
                continue
            a3 = ac.rearrange("p (t c) -> p t c", t=4)
            em.op("dve", lambda e, a3=a3, p4=p4: e.tensor_tensor(a3, p4[:, 0, :, :], p4[:, 1, :, :], ALU.add), reads=[ptb], writes=[acb])
            em.op("dve", lambda e, a3=a3, p4=p4: e.tensor_tensor(a3, a3, p4[:, 2, :, :], ALU.add), reads=[ptb, acb], writes=[acb])
            a4 = ac.rearrange("p (t h e) -> p t h e", t=4, e=65)
            r3 = rc.rearrange("p (t h) -> p t h", t=4)
            em.op("dve", lambda e, a4=a4, r3=r3: e.reciprocal(r3, a4[:, :, :, 64]), reads=[acb], writes=[rcb])
            o4 = on.rearrange("p (t h e) -> p t h e", t=4, e=64)
            em.op("dve", lambda e, a4=a4, r3=r3, o4=o4: e.tensor_tensor(o4, a4[:, :, :, 0:64], r3.unsqueeze(3).broadcast_to([128, 4, 8, 64]), ALU.mult), reads=[acb, rcb], writes=[onb])
            if MCUT < 3:
                continue
            o3 = on.rearrange("p (t f) -> p t f", t=4)
            oT3 = oT_.rearrange("p (j t) -> p j t", j=4)
            for j in range(4):
                bi = j
                pv = ps[bi].bitcast(BF16)[:, 0:512]
                for t in range(4):
                    em.op("pe", lambda e, pv=pv, t=t, j=j, o3=o3: e.transpose(pv[:, t * 128:(t + 1) * 128], o3[:, t, j * 128:(j + 1) * 128], self.cblk(0)), reads=[onb, self.cb], writes=[psb[bi]])
                em.op("dve", lambda e, pv=pv, j=j, oT3=oT3: e.tensor_copy(oT3[:, j, :], pv), reads=[psb[bi]], writes=[oTb_])
            em.dma("pool", lambda e, oT3=oT3, g0=g0: e.dma_start(self.oT[512:1024, g0:g0 + 512].rearrange("(j p) t -> p j t", p=128), oT3),
                   reads=[oTb_], writes=[self.db("oT", g0 // 512, "b")])

    def stage_G(self, si, tok0, S, l, hT, hTb):
        em = self.em
        ps, psb = self.ps, self.psb
        NB = S // 512
        wg = self.ring(2, 8 * 512, BF16, "wg")
        gt = self.ring(3, 512, BF16, "gt")
        k = 0
        for sl in range(6):
            wt, wtb = wg[sl % 2]
            w3 = wt.rearrange("p (k n) -> p k n", k=8)
            em.dma("sp", lambda e, w3=w3, sl=sl: e.dma_start(w3, self.wslab("w_in", l, OFF_G + sl * 512, 512, 8)), writes=[wtb])
            for sub in range(4):
                gc = sl * 4 + sub
                for blk in range(NB):
                    bi = k % 4
                    g_, gb_ = gt[k % 3]
                    k += 1
                    for kc in range(8):
                        em.op("pe", lambda e, w3=w3, kc=kc, sub=sub, blk=blk, bi=bi: e.matmul(ps[bi], w3[:, kc, sub * 128:(sub + 1) * 128], hT[:, kc, blk * 512:(blk + 1) * 512], start=(kc == 0), stop=(kc == 7)),
                              reads=[wtb, hTb[blk]], writes=[psb[bi]])
                    em.op("act", lambda e, g_=g_, bi=bi: e.activation(g_, ps[bi], AF.Sigmoid), reads=[psb[bi]], writes=[gb_])
                    g0 = tok0 + blk * 512
                    em.dma("pool", lambda e, g_=g_, gc=gc, g0=g0: e.dma_start(self.gT[gc * 128:(gc + 1) * 128, g0:g0 + 512], g_), reads=[gb_], writes=[self.db("gT", g0 // 512, gc)])

    def stage_M(self, si, tok0, S, l):
        for blk in range(S // 512):
            self.top = self.base_top
            self.m_block(si, tok0 + blk * 512, l)
            self.em.barrier()

    def m_block(self, si, g0, l):
        em = self.em
        ps, psb = self.ps, self.psb
        last = (l == self.nlayers - 1)
        src = self.x if l == 0 else self.xs1
        gb = g0 // 512
        xblk = self.alloc3(4, D, F32)
        xbb = [self.nb("xblk") for _ in range(4)]
        wring = self.ring(3, 8 * 512, BF16, "wm")
        tmpf = self.ring(3, 512, F32, "tmpf")
        hT2 = self.alloc3(8, 512, BF16)
        hT2b = self.nb("hT2")
        mT = self.alloc3(8, 512, BF16)
        mTb = self.nb("mT")
        mark = self.top
        wi = [0]
        ti = [0]

        def wload(name, c0, ncols, kc_n, r0=0):
            wt, wtb = wring[wi[0] % 3]
            wi[0] += 1
            w3 = wt[:, 0:kc_n * ncols].rearrange("p (k n) -> p k n", k=kc_n)
            em.dma("sp", lambda e: e.dma_start(w3, self.wslab(name, l, c0, ncols, kc_n, r0)), writes=[wtb])
            return w3, wtb

        def tmp():
            t = tmpf[ti[0] % 3]
            ti[0] += 1
            return t

        rd = [self.db("xs1", gb)] if l > 0 else []
        for t in range(4):
            em.dma("sp", lambda e, t=t: e.dma_start(xblk[:, t, :], src[g0 + t * 128:g0 + (t + 1) * 128, :]), reads=rd, writes=[xbb[t]])
        oTb_ = self.alloc3(16, 512, BF16)
        oTbb = self.nb("oTblk")
        gTb_ = self.alloc3(24, 512, BF16)
        gTbb = self.nb("gTblk")
        macc = self.alloc3(8, 512, F32)
        maccb = [self.nb("macc") for _ in range(8)]
        em.dma("sp", lambda e: e.dma_start(oTb_, self.oT[:, g0:g0 + 512].rearrange("(c p) t -> p c t", p=128)),
               reads=[b for k, b in em.bufs.items() if k[0] == "oT" and k[1] == gb], writes=[oTbb])
        em.dma("sp", lambda e: e.dma_start(gTb_, self.gT[:, g0:g0 + 512].rearrange("(c p) t -> p c t", p=128)),
               reads=[b for k, b in em.bufs.items() if k[0] == "gT" and k[1] == gb], writes=[gTbb])
        k = 0
        for br, (name, koff, kcn) in enumerate((("w_br_a", 0, 4), ("w_br_b", 4, 4), ("w_br_c", 8, 8))):
            for half in range(2):
                w3, wtb = wload(name, half * 512, 512, kcn)
                for sub in range(4):
                    n = half * 4 + sub
                    bi = k % 4
                    k += 1
                    for kc in range(kcn):
                        em.op("pe", lambda e, w3=w3, kc=kc, sub=sub, bi=bi, koff=koff, kcn=kcn: e.matmul(ps[bi], w3[:, kc, sub * 128:(sub + 1) * 128], oTb_[:, koff + kc, :], start=(kc == 0), stop=(kc == kcn - 1)),
                              reads=[wtb, oTbb], writes=[psb[bi]])
                    if br == 0:
                        em.op("dve", lambda e, n=n, bi=bi: e.tensor_tensor(macc[:, n, :], ps[bi], gTb_[:, n, :], ALU.mult), reads=[psb[bi], gTbb], writes=[maccb[n]])
                    else:
                        t_, tb_ = tmp()
                        em.op("dve", lambda e, n=n, bi=bi, br=br, t_=t_: e.tensor_tensor(t_, ps[bi], gTb_[:, br * 8 + n, :], ALU.mult), reads=[psb[bi], gTbb], writes=[tb_])
                        em.op("pool", lambda e, n=n, t_=t_: e.tensor_tensor(macc[:, n, :], macc[:, n, :], t_, ALU.add), reads=[tb_, maccb[n]], writes=[maccb[n]])
        for n in range(8):
            em.op("act", lambda e, n=n: e.copy(mT[:, n, :], macc[:, n, :]), reads=[maccb[n]], writes=[mTb])
        for half in range(2):
            w3, wtb = wload("w_out", half * 512, 512, 8)
            for t in range(4):
                bi = 4 + (half * 4 + t) % 2
                for kc in range(8):
                    em.op("pe", lambda e, w3=w3, kc=kc, t=t, bi=bi: e.matmul(ps[bi], mT[:, kc, t * 128:(t + 1) * 128], w3[:, kc, :], start=(kc == 0), stop=(kc == 7)),
                          reads=[wtb, mTb], writes=[psb[bi]])
                em.op("dve", lambda e, t=t, half=half, bi=bi: e.tensor_tensor(xblk[:, t, half * 512:(half + 1) * 512], xblk[:, t, half * 512:(half + 1) * 512], ps[bi], ALU.add),
                      reads=[psb[bi], xbb[t]], writes=[xbb[t]])
        em.barrier()
        self.top = mark
        o = l * SP_PER_LAYER
        for t in range(4):
            self.norm_tile(xblk[:, t, :], xbb[t], self.spf[:, o + SP_GFFN:o + SP_GFFN + 8], hT2[:, :, t * 128:(t + 1) * 128], hT2b)
        aT = self.alloc3(22, 512, BF16)
        aTb = self.nb("aT")
        k = 0
        for js in range(6):
            ncols = min(512, FFN - js * 512)
            wg3, wgb = wload("w_ffn_in", js * 512, ncols, 8)
            wu3, wub = wload("w_ffn_in", FFN + js * 512, ncols, 8)
            for sub in range(ncols // 128):
                j = js * 4 + sub
                bg, bu = (k % 2) * 2, (k % 2) * 2 + 1
                k += 1
                for kc in range(8):
                    em.op("pe", lambda e, wg3=wg3, kc=kc, sub=sub, bg=bg: e.matmul(ps[bg], wg3[:, kc, sub * 128:(sub + 1) * 128], hT2[:, kc, :], start=(kc == 0), stop=(kc == 7)), reads=[wgb, hT2b], writes=[psb[bg]])
                for kc in range(8):
                    em.op("pe", lambda e, wu3=wu3, kc=kc, sub=sub, bu=bu: e.matmul(ps[bu], wu3[:, kc, sub * 128:(sub + 1) * 128], hT2[:, kc, :], start=(kc == 0), stop=(kc == 7)), reads=[wub, hT2b], writes=[psb[bu]])
                t_, tb_ = tmp()
                em.op("act", lambda e, t_=t_, bg=bg: e.activation(t_, ps[bg], AF.Silu), reads=[psb[bg]], writes=[tb_])
                em.op("dve", lambda e, t_=t_, bu=bu, j=j: e.tensor_tensor(aT[:, j, :], t_, ps[bu], ALU.mult), reads=[tb_, psb[bu]], writes=[aTb])
        wfo = self.alloc3(22, 512, BF16)
        wfob = self.nb("wfo")
        for half in range(2):
            em.dma("sp", lambda e, half=half: e.dma_start(wfo, self.wslab("w_ffn_out", l, half * 512, 512, 22)), writes=[wfob])
            for t in range(4):
                bi = 4 + (half * 4 + t) % 2
                for j in range(22):
                    em.op("pe", lambda e, j=j, t=t, bi=bi: e.matmul(ps[bi], aT[:, j, t * 128:(t + 1) * 128], wfo[:, j, :], start=(j == 0), stop=(j == 21)), reads=[wfob, aTb], writes=[psb[bi]])
                em.op("dve", lambda e, t=t, half=half, bi=bi: e.tensor_tensor(xblk[:, t, half * 512:(half + 1) * 512], xblk[:, t, half * 512:(half + 1) * 512], ps[bi], ALU.add),
                      reads=[psb[bi], xbb[t]], writes=[xbb[t]])
        em.barrier()
        self.top = mark
        for t in range(4):
            self.norm_tile(xblk[:, t, :], xbb[t], self.spf[:, o + SP_GPLE:o + SP_GPLE + 8], hT2[:, :, t * 128:(t + 1) * 128], hT2b)
        pf = self.alloc3(4, 256, F32)
        pfb = self.nb("pf")
        pb16 = self.alloc3(4, 256, BF16)
        pbb = self.nb("pb16")
        pT = self.alloc3(2, 512, BF16)
        pTb = self.nb("pT")
        em.dma("sp", lambda e: e.dma_start(pf, self.p[l, g0:g0 + 512, :].rearrange("(t q) c -> q t c", q=128)), writes=[pfb])
        em.op("dve", lambda e: e.tensor_copy(pb16, pf), reads=[pfb], writes=[pbb])
        for kc in range(2):
            pv = ps[kc].bitcast(BF16)[:, 0:512]
            for t in range(4):
                em.op("pe", lambda e, pv=pv, t=t, kc=kc: e.transpose(pv[:, t * 128:(t + 1) * 128], pb16[:, t, kc * 128:(kc + 1) * 128], self.cblk(0)), reads=[pbb, self.cb], writes=[psb[kc]])
            em.op("dve", lambda e, pv=pv, kc=kc: e.tensor_copy(pT[:, kc, :], pv), reads=[psb[kc]], writes=[pTb])
        for half in range(2):
            wg3, wgb = wload("w_ple_gate", half * 512, 512, 8)
            wp3, wpb = wload("w_ple_proj", half * 512, 512, 2)
            for t in range(4):
                bg, bp = 2 + (t % 2) * 2, 3 + (t % 2) * 2
                for kc in range(8):
                    em.op("pe", lambda e, wg3=wg3, kc=kc, t=t, bg=bg: e.matmul(ps[bg], hT2[:, kc, t * 128:(t + 1) * 128], wg3[:, kc, :], start=(kc == 0), stop=(kc == 7)), reads=[wgb, hT2b], writes=[psb[bg]])
                for kc in range(2):
                    em.op("pe", lambda e, wp3=wp3, kc=kc, t=t, bp=bp: e.matmul(ps[bp], pT[:, kc, t * 128:(t + 1) * 128], wp3[:, kc, :], start=(kc == 0), stop=(kc == 1)), reads=[wpb, pTb], writes=[psb[bp]])
                t_, tb_ = tmp()
                em.op("act", lambda e, t_=t_, bg=bg: e.activation(t_, ps[bg], AF.Sigmoid), reads=[psb[bg]], writes=[tb_])
                em.op("dve", lambda e, t_=t_, bp=bp: e.tensor_tensor(t_, t_, ps[bp], ALU.mult), reads=[tb_, psb[bp]], writes=[tb_])
                em.op("pool", lambda e, t_=t_, t=t, half=half: e.tensor_tensor(xblk[:, t, half * 512:(half + 1) * 512], xblk[:, t, half * 512:(half + 1) * 512], t_, ALU.add), reads=[tb_, xbb[t]], writes=[xbb[t]])
        if not last:
            for t in range(4):
                em.dma("pool", lambda e, t=t: e.dma_start(self.xs1[g0 + t * 128:g0 + (t + 1) * 128, :], xblk[:, t, :]), reads=[xbb[t]], writes=[self.db("xs1", gb)])
        else:
            yt = self.ring(2, D, F32, "yt")
            gfin = self.rpf[:, RP_FIN:RP_FIN + D]
            for t in range(4):
                i = self.n_i
                self.n_i += 1
                stt, stb = self.n_st[i % 2]
                y_, yb_ = yt[t % 2]
                jb = self.db("junk")
                em.op("act", lambda e, t=t, stt=stt: e.activation(self.n_junk, xblk[:, t, :], AF.Square, accum_out=stt[:, 0:1]), reads=[xbb[t]], writes=[jb, stb])
                em.op("dve", lambda e, stt=stt: e.tensor_scalar(stt[:, 1:2], stt[:, 0:1], 1.0 / D, EPS, ALU.mult, ALU.add), reads=[stb], writes=[stb])
                em.op("act", lambda e, stt=stt: e.activation(stt[:, 2:3], stt[:, 1:2], AF.Sqrt), reads=[stb], writes=[stb])
                em.op("dve", lambda e, stt=stt: e.reciprocal(stt[:, 3:4], stt[:, 2:3]), reads=[stb], writes=[stb])
                em.op("dve", lambda e, t=t, stt=stt, y_=y_: e.scalar_tensor_tensor(y_, xblk[:, t, :], stt[:, 3:4], gfin, ALU.mult, ALU.mult), reads=[xbb[t], stb, self.cb], writes=[yb_])
                em.dma("pool", lambda e, t=t, y_=y_: e.dma_start(self.y[g0 + t * 128:g0 + (t + 1) * 128, :], y_), reads=[yb_], writes=[self.db("y", gb, t)])

    def stage_Cprep(self, si, tok0, S, l, hT, hTb):
        em = self.em
        ps, psb = self.ps, self.psb
        NT, NB = S // 128, S // 512
        o = l * SP_PER_LAYER
        r_ = l * RP_PER_LAYER
        diag = self.alloc(60 * 128, BF16).rearrange("p (c j n) -> p c j n", c=12, j=5)
        diagb = self.nb("diag")
        for cc in range(12):
            for j in range(5):
                col = o + SP_CONVW + cc * 5 + j
                em.op("dve", lambda e, cc=cc, j=j, col=col: e.tensor_scalar(diag[:, cc, j, :], self.cblk(0, bf=False), self.spf[:, col:col + 1], None, ALU.mult), reads=[self.cb], writes=[diagb])
        xraw = self.alloc(S + 4, BF16)
        xrb = [self.nb("xraw") for _ in range(NB)]
        em.op("pool", lambda e: e.memset(xraw[:, 0:2], 0.0), writes=[xrb[0]])
        em.op("pool", lambda e: e.memset(xraw[:, S + 2:S + 4], 0.0), writes=[xrb[NB - 1]])
        wx = self.ring(2, 8 * 128, BF16, "wx")
        xc = self.ring(3, 512, BF16, "xc")
        xst = self.ring(2, 512, BF16, "xst")
        k = [0]

        def chunk(cc):
            wt, wtb = wx[cc % 2]
            w3 = wt.rearrange("p (k n) -> p k n", k=8)
            em.dma("sp", lambda e: e.dma_start(w3, self.wslab("w_in", l, OFF_XBC + cc * 128, 128, 8)), writes=[wtb])
            for blk in range(NB):
                bi = blk % 2
                for kc in range(8):
                    em.op("pe", lambda e, kc=kc, blk=blk, bi=bi: e.matmul(ps[bi], w3[:, kc, :], hT[:, kc, blk * 512:(blk + 1) * 512], start=(kc == 0), stop=(kc == 7)), reads=[wtb, hTb[blk]], writes=[psb[bi]])
                em.op("act", lambda e, blk=blk, bi=bi: e.copy(xraw[:, 2 + blk * 512:2 + (blk + 1) * 512], ps[bi]), reads=[psb[bi]], writes=[xrb[blk]])
            for blk in range(NB):
                bi = 2 + blk % 2
                g0 = tok0 + blk * 512
                for j in range(5):
                    em.op("pe", lambda e, j=j, blk=blk, bi=bi: e.matmul(ps[bi], diag[:, cc, j, :], xraw[:, blk * 512 + j:blk * 512 + j + 512], start=(j == 0), stop=(j == 4)),
                          reads=[diagb, xrb[max(0, blk - 1)], xrb[blk], xrb[min(NB - 1, blk + 1)]], writes=[psb[bi]])
                x_, xb_ = xc[k[0] % 3]
                k[0] += 1
                col = o + SP_CONVB + cc
                em.op("act", lambda e, x_=x_, bi=bi, col=col: e.activation(x_, ps[bi], AF.Silu, bias=self.spf[:, col:col + 1]), reads=[psb[bi], self.cb], writes=[xb_])
                if cc >= 8:
                    em.dma("pool", lambda e, x_=x_, g0=g0: e.dma_start(self.bct[(cc - 8) * 128:(cc - 7) * 128, g0:g0 + 512], x_), reads=[xb_], writes=[self.db("bct", cc, g0 // 512)])
                if cc < 10:
                    bt = 4 + blk % 2
                    pv = ps[bt].bitcast(BF16)[:, 0:512]
                    for t in range(4):
                        em.op("pe", lambda e, pv=pv, t=t, x_=x_, bt=bt: e.transpose(pv[:, t * 128:(t + 1) * 128], x_[:, t * 128:(t + 1) * 128], self.cblk(0)), reads=[xb_, self.cb], writes=[psb[bt]])
                    s_, sb_ = xst[k[0] % 2]
                    em.op("dve", lambda e, pv=pv, s_=s_: e.tensor_copy(s_, pv), reads=[psb[bt]], writes=[sb_])
                    if cc < 8:
                        dstv = self.xstok[g0:g0 + 512, cc * 128:(cc + 1) * 128].rearrange("(t q) c -> q t c", q=128)
                        key = self.db("xstok", cc, g0 // 512)
                    else:
                        dstv = self.btok[g0:g0 + 512, (cc - 8) * 128:(cc - 7) * 128].rearrange("(t q) c -> q t c", q=128)
                        key = self.db("btok", cc, g0 // 512)
                    em.dma("pool", lambda e, dstv=dstv, s_=s_: e.dma_start(dstv, s_.rearrange("p (t c) -> p t c", t=4)), reads=[sb_], writes=[key])
        for cc in range(12):
            chunk(cc)
        wz = self.ring(2, 8 * 512, BF16, "wz")
        szt = self.ring(3, 512, BF16, "szt")
        kk = 0
        for half in range(2):
            wt, wtb = wz[half]
            w3 = wt.rearrange("p (k n) -> p k n", k=8)
            em.dma("sp", lambda e, w3=w3, half=half: e.dma_start(w3, self.wslab("w_in", l, OFF_Z + half * 512, 512, 8)), writes=[wtb])
            for tt in range(NT):
                bi = kk % 4
                s_, sb_ = szt[kk % 3]
                kk += 1
                for kc in range(8):
                    em.op("pe", lambda e, w3=w3, kc=kc, tt=tt, bi=bi: e.matmul(ps[bi], hT[:, kc, tt * 128:(tt + 1) * 128], w3[:, kc, :], start=(kc == 0), stop=(kc == 7)), reads=[wtb, hTb[tt // 4]], writes=[psb[bi]])
                em.op("act", lambda e, s_=s_, bi=bi: e.activation(s_, ps[bi], AF.Silu), reads=[psb[bi]], writes=[sb_])
                t0 = tok0 + tt * 128
                em.dma("pool", lambda e, s_=s_, t0=t0, half=half: e.dma_start(self.szd[t0:t0 + 128, half * 512:(half + 1) * 512], s_), reads=[sb_], writes=[self.db("szd", t0 // 128, half)])
        wdt = self.alloc3(8, 32, BF16)
        wdtb = self.nb("wdt")
        em.dma("sp", lambda e: e.dma_start(wdt, self.wslab("w_in", l, OFF_DT, 32, 8)), writes=[wdtb])
        dtb_ = self.nb("dt")
        self.dtb_ = dtb_
        tmpd = self.alloc3(16, 32, F32)
        tmpb = self.nb("tmpd")
        for t16 in range(0, NT, 16):
            n = min(16, NT - t16)
            bi = 6 + (t16 // 16) % 2
            for ti in range(n):
                tt = t16 + ti
                for kc in range(8):
                    em.op("pe", lambda e, kc=kc, tt=tt, ti=ti, bi=bi: e.matmul(ps[bi][:, ti * 32:(ti + 1) * 32], hT[:, kc, tt * 128:(tt + 1) * 128], wdt[:, kc, :], start=(kc == 0), stop=(kc == 7)), reads=[wdtb, hTb[tt // 4]], writes=[psb[bi]])
            bias = self.rpf[:, r_ + RP_DTB:r_ + RP_DTB + 32].unsqueeze(1).broadcast_to([128, n, 32])
            em.op("dve", lambda e, n=n, bi=bi, bias=bias: e.tensor_tensor(tmpd[:, 0:n, :], ps[bi][:, 0:n * 32].rearrange("p (t c) -> p t c", c=32), bias, ALU.add), reads=[psb[bi], self.cb], writes=[tmpb])
            em.op("act", lambda e, n=n: e.activation(tmpd[:, 0:n, :], tmpd[:, 0:n, :], AF.Exp), reads=[tmpb], writes=[tmpb])
            em.op("act", lambda e, n=n, t16=t16: e.activation(self.dt_[:, t16:t16 + n, :], tmpd[:, 0:n, :], AF.Ln, bias=1.0), reads=[tmpb], writes=[dtb_])

    def stage_Cscan(self, si, tok0, S, l):
        em = self.em
        ps, psb = self.ps, self.psb
        NT = S // 128
        o = l * SP_PER_LAYER
        r_ = l * RP_PER_LAYER
        dt = self.dt_
        dtb_ = self.dtb_
        sel = self.alloc(32 * 128, BF16)[0:32, :]
        selb = self.nb("sel")
        em.dma("pool", lambda e: e.dma_start(sel, self.sel32d[:, :]), writes=[selb])
        adt = self.alloc3(NT, 32, F32)
        cs = self.alloc3(NT, 32, F32)
        ecs = self.alloc3(NT, 32, F32)
        w2 = self.alloc3(NT, 32, F32)
        cd = self.alloc3(NT, 32, F32)
        csT = self.alloc(NT * 128, F32)[0:32, :]
        smb = self.nb("small")
        csTb = self.nb("csT")
        csH = self.alloc(NT * 128, BF16)[0:32, :]
        csL = self.alloc(NT * 128, BF16)[0:32, :]
        csR = self.alloc(NT * 128, F32)[0:32, :]
        an = self.aneg[:, l * 32:(l + 1) * 32].unsqueeze(1).broadcast_to([128, NT, 32])
        em.op("dve", lambda e: e.tensor_tensor(adt, dt, an, ALU.mult), reads=[dtb_, self.cb], writes=[smb])
        tri_le, tri_ge, sel_last, sel_first, ident_f = (self.cblk(i, bf=False) for i in (6, 7, 8, 9, 0))
        for t16 in range(0, NT, 16):
            n = min(16, NT - t16)
            bi = (t16 // 16) % 2
            for ti in range(n):
                tt = t16 + ti
                em.op("pe", lambda e, tt=tt, ti=ti, bi=bi: e.matmul(ps[bi][:, ti * 32:ti * 32 + 16], tri_le, adt[:, tt, 0:16], start=True, stop=True), reads=[smb, self.cb], writes=[psb[bi]])
                em.op("pe", lambda e, tt=tt, ti=ti, bi=bi: e.matmul(ps[bi][:, ti * 32 + 16:ti * 32 + 32], tri_ge, adt[:, tt, 16:32], start=True, stop=True), reads=[smb, self.cb], writes=[psb[bi]])
            em.op("dve", lambda e, n=n, bi=bi, t16=t16: e.tensor_copy(cs[:, t16:t16 + n, :], ps[bi][:, 0:n * 32].rearrange("p (t c) -> p t c", c=32)), reads=[psb[bi]], writes=[smb])
            b2 = 2 + (t16 // 16) % 2
            for ti in range(n):
                tt = t16 + ti
                em.op("pe", lambda e, tt=tt, ti=ti, b2=b2: e.matmul(ps[b2][:, ti * 32:ti * 32 + 16], sel_last, cs[:, tt, 0:16], start=True, stop=True), reads=[smb, self.cb], writes=[psb[b2]])
                em.op("pe", lambda e, tt=tt, ti=ti, b2=b2: e.matmul(ps[b2][:, ti * 32 + 16:ti * 32 + 32], sel_first, cs[:, tt, 16:32], start=True, stop=True), reads=[smb, self.cb], writes=[psb[b2]])
            lv = lambda b2=b2, n=n: ps[b2][:, 0:n * 32].rearrange("p (t c) -> p t c", c=32)
            em.op("dve", lambda e, n=n, t16=t16, lv=lv: e.tensor_tensor(w2[:, t16:t16 + n, :], lv(), cs[:, t16:t16 + n, :], ALU.subtract), reads=[psb[b2], smb], writes=[smb])
            em.op("act", lambda e, n=n, t16=t16: e.activation(w2[:, t16:t16 + n, :], w2[:, t16:t16 + n, :], AF.Exp), reads=[smb], writes=[smb])
            em.op("act", lambda e, n=n, t16=t16, lv=lv: e.activation(cd[:, t16:t16 + n, :], lv(), AF.Exp), reads=[psb[b2]], writes=[smb])
        em.op("dve", lambda e: e.tensor_tensor(w2, w2, dt, ALU.mult), reads=[smb, dtb_], writes=[smb])
        em.op("act", lambda e: e.activation(ecs, cs, AF.Exp), reads=[smb], writes=[smb])
        for t4 in range(0, NT, 4):
            bi = 4 + (t4 // 4) % 2
            for ti in range(4):
                tt = t4 + ti
                em.op("pe", lambda e, tt=tt, ti=ti, bi=bi: e.transpose(ps[bi][0:32, ti * 128:(ti + 1) * 128], cs[:, tt, :], ident_f), reads=[smb, self.cb], writes=[psb[bi]])
            em.op("dve", lambda e, t4=t4, bi=bi: e.tensor_copy(csT[:, t4 * 128:(t4 + 4) * 128], ps[bi][0:32, :]), reads=[psb[bi]], writes=[csTb])
        em.op("dve", lambda e: e.tensor_copy(csH, csT), reads=[csTb], writes=[csTb])
        em.op("dve", lambda e: e.tensor_tensor(csR, csT, csH, ALU.subtract), reads=[csTb], writes=[csTb])
        em.op("dve", lambda e: e.tensor_copy(csL, csR), reads=[csTb], writes=[csTb])
        dsk = self.rpf[:, r_ + RP_SSMD:r_ + RP_SSMD + 16]
        H = self.alloc(512, F32)
        Hb = self.nb("H")
        Hbf = self.ring(2, 512, BF16, "Hbf")
        xs_r = self.ring(2, 512, BF16, "xs")
        bt_r = self.ring(2, 128, BF16, "btok")
        BT_r = self.ring(2, 128, BF16, "BT")
        CT_r = self.ring(2, 128, BF16, "CT")
        hb_r = self.ring(2, 512, BF16, "hbl")
        sz_r = self.ring(2, 512, BF16, "sz")
        Xd_r = self.ring(3, 512, BF16, "Xd")
        Gs_r = self.ring(2, 128, BF16, "Gs")
        Dm_r = self.ring(8, 128, F32, "Dm")
        Lm_r = self.ring(8, 128, BF16, "Lm")
        Mm_r = self.ring(8, 128, BF16, "Mm")
        tf2 = [self.ring(6, 512, F32, "tf") for _ in range(2)]
        yn_r = self.ring(2, 512, BF16, "yn")
        oc_r = self.ring(2, 4 * 512, BF16, "ocT")
        st_r = self.ring(2, 4, F32, "cst")
        negm = (self.cblk(10, bf=False), self.cblk(11, bf=False))
        ctr = {"x": 0, "m": 0, "t": 0}

        def loads(gi, c, full):
            t0 = tok0 + c * 128
            i = ctr["x"]
            ctr["x"] += 1
            xs_, xsb = xs_r[i % 2]
            bt_, btb = bt_r[i % 2]
            em.dma("sp", lambda e: e.dma_start(xs_, self.xstok[t0:t0 + 128, gi * 512:(gi + 1) * 512]), reads=[self.db("xstok", gi * 4 + j, t0 // 512) for j in range(4)], writes=[xsb])
            em.dma("sp", lambda e: e.dma_start(bt_, self.btok[t0:t0 + 128, gi * 128:(gi + 1) * 128]), reads=[self.db("btok", 8 + gi, t0 // 512)], writes=[btb])
            if not full:
                return (xs_, xsb), (bt_, btb)
            BT_, BTb = BT_r[i % 2]
            CT_, CTb = CT_r[i % 2]
            hb_, hbb = hb_r[i % 2]
            sz_, szb = sz_r[i % 2]
            em.dma("sp", lambda e: e.dma_start(BT_, self.bct[gi * 128:(gi + 1) * 128, t0:t0 + 128]), reads=[self.db("bct", 8 + gi, t0 // 512)], writes=[BTb])
            em.dma("sp", lambda e: e.dma_start(CT_, self.bct[(2 + gi) * 128:(3 + gi) * 128, t0:t0 + 128]), reads=[self.db("bct", 10 + gi, t0 // 512)], writes=[CTb])
            em.dma("sp", lambda e: e.dma_start(hb_, self.hbd[gi, t0 // 128, :, :]), reads=[self.db("hbd", gi, t0 // 128)], writes=[hbb])
            em.dma("sp", lambda e: e.dma_start(sz_, self.szd[t0:t0 + 128, gi * 512:(gi + 1) * 512]), reads=[self.db("szd", t0 // 128, gi)], writes=[szb])
            return (xs_, xsb), (bt_, btb), (BT_, BTb), (CT_, CTb), (hb_, hbb), (sz_, szb)

        def bc8(arr, c, col0):
            return arr[:, c, col0:col0 + 8].unsqueeze(2).broadcast_to([128, 8, 64])

        def v3(a):
            return a.rearrange("p (e d) -> p e d", e=8)

        def state_update(gi, c, d, xs_, xsb, bt_, btb):
            col0 = d * 16 + gi * 8
            Xd, Xdb = Xd_r[ctr["m"] % 3]
            ctr["m"] += 1
            em.op("dve", lambda e: e.tensor_tensor(v3(Xd), v3(xs_), bc8(w2, c, col0), ALU.mult), reads=[xsb, smb], writes=[Xdb])
            em.op("pe", lambda e: e.matmul(ps[6], bt_, Xd, start=True, stop=True), reads=[btb, Xdb], writes=[psb[6]])
            em.op("dve", lambda e: e.tensor_tensor(v3(H), v3(H), bc8(cd, c, col0), ALU.mult), reads=[Hb, smb], writes=[Hb])
            em.op("dve", lambda e: e.tensor_tensor(H, H, ps[6], ALU.add), reads=[Hb, psb[6]], writes=[Hb])

        def bwd_chunk(gi, c):
            (xs_, xsb), (bt_, btb) = loads(gi, c, False)
            hbf, hbfb = Hbf[c % 2]
            em.op("act", lambda e: e.copy(hbf, H), reads=[Hb], writes=[hbfb])
            em.dma("pool", lambda e: e.dma_start(self.hbd[gi, (tok0 // 128) + c, :, :], hbf), reads=[hbfb], writes=[self.db("hbd", gi, (tok0 // 128) + c)])
            state_update(gi, c, 1, xs_, xsb, bt_, btb)

        def fwd_chunk(gi, c):
            (xs_, xsb), (bt_, btb), (BT_, BTb), (CT_, CTb), (hb_, hbb), (sz_, szb) = loads(gi, c, True)
            hbf, hbfb = Hbf[c % 2]
            em.op("act", lambda e: e.copy(hbf, H), reads=[Hb], writes=[hbfb])
            Gs, Gsb = Gs_r[c % 2]
            em.op("pe", lambda e: e.matmul(ps[0][:, 0:128], BT_, CT_, start=True, stop=True), reads=[BTb, CTb], writes=[psb[0]])
            em.op("act", lambda e: e.copy(Gs, ps[0][:, 0:128]), reads=[psb[0]], writes=[Gsb])
            X = []
            for d in range(2):
                Xd, Xdb = Xd_r[ctr["m"] % 3]
                ctr["m"] += 1
                em.op("dve", lambda e, Xd=Xd, d=d: e.tensor_tensor(v3(Xd), v3(xs_), bc8(dt, c, d * 16 + gi * 8), ALU.mult), reads=[xsb, dtb_], writes=[Xdb])
                X.append((Xd, Xdb))
            for rnd in range(2):
                combos = [(e_, d) for e_ in range(rnd * 4, rnd * 4 + 4) for d in range(2)]
                for i, (e_, d) in enumerate(combos):
                    col = d * 16 + gi * 8 + e_
                    rbv = ps[1 + i // 4][:, (i % 4) * 128:(i % 4 + 1) * 128]
                    em.op("pe", lambda e, rbv=rbv, col=col: e.matmul(rbv, sel[:, col * 128:(col + 1) * 128], csH[:, c * 128:(c + 1) * 128], start=True, stop=False), reads=[selb, csTb], writes=[psb[1 + i // 4]])
                    em.op("pe", lambda e, rbv=rbv, col=col: e.matmul(rbv, sel[:, col * 128:(col + 1) * 128], csL[:, c * 128:(c + 1) * 128], start=False, stop=True), reads=[selb, csTb], writes=[psb[1 + i // 4]])
                for i, (e_, d) in enumerate(combos):
                    col = d * 16 + gi * 8 + e_
                    rbv = ps[1 + i // 4][:, (i % 4) * 128:(i % 4 + 1) * 128]
                    Dm, Dmb = Dm_r[i]
                    Lm, Lmb = Lm_r[i]
                    Mm, Mmb = Mm_r[i]
                    em.op("dve", lambda e, rbv=rbv, col=col, Dm=Dm, d=d: e.scalar_tensor_tensor(Dm, rbv, cs[:, c, col:col + 1], negm[d], ALU.subtract, ALU.add), reads=[psb[1 + i // 4], smb, self.cb], writes=[Dmb])
                    em.op("act", lambda e, Dm=Dm, Lm=Lm: e.activation(Lm, Dm, AF.Exp), reads=[Dmb], writes=[Lmb])
                    em.op("pool", lambda e, Lm=Lm, Mm=Mm: e.tensor_tensor(Mm, Lm, Gs, ALU.mult), reads=[Lmb, Gsb], writes=[Mmb])
                for i, (e_, d) in enumerate(combos):
                    Mm, Mmb = Mm_r[i]
                    Xd, Xdb = X[d]
                    em.op("pe", lambda e, Mm=Mm, Xd=Xd, e_=e_, d=d: e.matmul(ps[3][:, e_ * 64:(e_ + 1) * 64], Mm, Xd[:, e_ * 64:(e_ + 1) * 64], start=(d == 0), stop=(d == 1)), reads=[Mmb, Xdb], writes=[psb[3]])
            em.op("pe", lambda e: e.matmul(ps[4], CT_, hbf, start=True, stop=True), reads=[CTb, hbfb], writes=[psb[4]])
            em.op("pe", lambda e: e.matmul(ps[5], CT_, hb_, start=True, stop=True), reads=[CTb, hbb], writes=[psb[5]])
            (t1, t1b), (t2, t2b), (t3, t3b), (s1, s1b), (s2, s2b), (yz, yzb) = tf2[c % 2]
            em.op("dve", lambda e: e.tensor_tensor(v3(t1), v3(ps[4]), bc8(ecs, c, gi * 8), ALU.mult), reads=[psb[4], smb], writes=[t1b])
            em.op("dve", lambda e: e.tensor_tensor(v3(t2), v3(ps[5]), bc8(ecs, c, 16 + gi * 8), ALU.mult), reads=[psb[5], smb], writes=[t2b])
            dk = dsk[:, gi * 8:gi * 8 + 8].unsqueeze(2).broadcast_to([128, 8, 64])
            em.op("pool", lambda e: e.tensor_tensor(v3(t3), v3(xs_), dk, ALU.mult), reads=[xsb, self.cb], writes=[t3b])
            em.op("dve", lambda e: e.tensor_tensor(s1, ps[3], t1, ALU.add), reads=[psb[3], t1b], writes=[s1b])
            em.op("pool", lambda e: e.tensor_tensor(s2, t2, t3, ALU.add), reads=[t2b, t3b], writes=[s2b])
            em.op("pool", lambda e: e.tensor_tensor(s1, s1, s2, ALU.add), reads=[s1b, s2b], writes=[s1b])
            em.op("dve", lambda e: e.tensor_tensor(yz, s1, sz_, ALU.mult), reads=[s1b, szb], writes=[yzb])
            stt, stb = st_r[c % 2]
            jb = self.db("junk")
            em.op("act", lambda e: e.activation(self.n_junk[:, 0:512], yz, AF.Square, accum_out=stt[:, 0:1]), reads=[yzb], writes=[jb, stb])
            em.op("dve", lambda e: e.tensor_scalar(stt[:, 1:2], stt[:, 0:1], 1.0 / 512, EPS, ALU.mult, ALU.add), reads=[stb], writes=[stb])
            em.op("act", lambda e: e.activation(stt[:, 2:3], stt[:, 1:2], AF.Sqrt), reads=[stb], writes=[stb])
            em.op("dve", lambda e: e.reciprocal(stt[:, 3:4], stt[:, 2:3]), reads=[stb], writes=[stb])
            yn, ynb = yn_r[c % 2]
            em.op("dve", lambda e: e.tensor_scalar(yn, yz, stt[:, 3:4], None, ALU.mult), reads=[yzb, stb], writes=[ynb])
            oc, ocb = oc_r[(c // 4) % 2]
            oc3 = oc.rearrange("p (j t) -> p j t", j=4)
            pv = ps[7].bitcast(BF16)[:, 0:512]
            for j in range(4):
                em.op("pe", lambda e, j=j: e.transpose(pv[:, j * 128:(j + 1) * 128], yn[:, j * 128:(j + 1) * 128], self.cblk(0)), reads=[ynb, self.cb], writes=[psb[7]])
            g4 = self.spf[:, o + SP_GSSM + gi * 4:o + SP_GSSM + gi * 4 + 4].unsqueeze(2).broadcast_to([128, 4, 128])
            tl = c % 4
            em.op("dve", lambda e: e.tensor_tensor(oc3[:, :, tl * 128:(tl + 1) * 128], pv.rearrange("p (j n) -> p j n", j=4), g4, ALU.mult), reads=[psb[7], self.cb], writes=[ocb])
            if tl == 3:
                g0 = tok0 + (c - 3) * 128
                em.dma("pool", lambda e: e.dma_start(self.oT[(8 + gi * 4) * 128:(12 + gi * 4) * 128, g0:g0 + 512].rearrange("(j p) t -> p j t", p=128), oc3), reads=[ocb], writes=[self.db("oT", g0 // 512, "c%d" % gi)])
            state_update(gi, c, 0, xs_, xsb, bt_, btb)

        for gi in range(2):
            em.op("pool", lambda e: e.memset(H, 0.0), writes=[Hb])
            for c in range(NT - 1, -1, -1):
                bwd_chunk(gi, c)
            em.op("pool", lambda e: e.memset(H, 0.0), writes=[Hb])
            for c in range(NT):
                fwd_chunk(gi, c)


SEQS = (2048, 2048, 4096, 4096)
N_CORES = 8


def kernel(**inputs):
    inp = {k: np.asarray(v) for k, v in inputs.items()}
    xp, xs = inp["x_prompt"], inp["x_sample"]
    pp, psm = inp["p_prompt"], inp["p_sample"]
    cpack, cosT, sinT, sel32 = make_consts()
    sp, rp = make_packs(inp)
    b = Builder(list(SEQS))
    nc = b.build()
    in_maps = []
    for i in range(N_CORES):
        x = np.concatenate([xp[2 * i], xp[2 * i + 1], xs[2 * i], xs[2 * i + 1]], axis=0)
        p = np.concatenate([pp[:, 2 * i], pp[:, 2 * i + 1], psm[:, 2 * i], psm[:, 2 * i + 1]], axis=1)
        m = {"x": np.ascontiguousarray(x, dtype=np.float32), "p": np.ascontiguousarray(p, dtype=np.float32),
             "cpack": cpack, "cosT": cosT, "sinT": sinT, "sel32": sel32, "spack": sp, "rpack": rp}
        for name, k, n in WEIGHTS:
            m[name] = np.ascontiguousarray(inp[name], dtype=np.float32)
        in_maps.append(m)
    res = run_bass_kernel_spmd(nc, in_maps, core_ids=list(range(N_CORES)))
    y_prompt = np.empty(xp.shape, np.float32)
    y_sample = np.empty(xs.shape, np.float32)
    for i in range(N_CORES):
        y = np.asarray(res.results[i]["y"], dtype=np.float32)
        y_prompt[2 * i] = y[0:2048]
        y_prompt[2 * i + 1] = y[2048:4096]
        y_sample[2 * i] = y[4096:8192]
        y_sample[2 * i + 1] = y[8192:12288]
    return (y_prompt, y_sample)
```

```python
import contextlib, math, os
import numpy as np
import concourse.bass as bass
import concourse.mybir as mybir
from concourse.bass_utils import run_bass_kernel_spmd

F32 = mybir.dt.float32
BF16 = mybir.dt.bfloat16
AF = mybir.ActivationFunctionType
ALU = mybir.AluOpType

D = 1024
WIN = 11808
FFN = 2816
DEPTH = 2
EPS = 1e-6
OFF_AQ, OFF_AK, OFF_AV, OFF_BQ, OFF_BK, OFF_BV, OFF_Z, OFF_XBC, OFF_DT, OFF_G = 0, 512, 1024, 1536, 3072, 4608, 6144, 7168, 8704, 8736
DILS = (1, 4, 16)
NEGBIG = -30000.0
SMAX = 4096

class Buf:
    __slots__ = ("name", "lw", "rd", "excl")
    def __init__(self, name, excl=False):
        self.name = name
        self.lw = None
        self.rd = {}
        self.excl = excl


class Em:
    ENG = ("pe", "act", "dve", "pool", "sp")
    NDMA = 24

    def __init__(self, nc):
        self.nc = nc
        self.lists = {e: [] for e in self.ENG}
        self.cnt = {e: 0 for e in self.ENG}
        self.seen = {e: {} for e in self.ENG}
        self.dma_i = {"sp": 0, "act": 0, "pool": 0}
        self.dma_used = set()
        self.nins = 0
        self.bufs = {}

    def buf(self, key):
        b = self.bufs.get(key)
        if b is None:
            b = Buf(key)
            self.bufs[key] = b
        return b

    def _need(self, eng, deps):
        waits = []
        seen = self.seen[eng]
        for k, v in deps.items():
            if eng == "pe" and k == ("c", "pe"):
                continue
            if seen.get(k, 0) >= v:
                continue
            seen[k] = v
            waits.append((k, v))
        return waits

    def _deps(self, reads, writes):
        deps = {}
        for b in reads:
            if b.lw is not None:
                k, v = b.lw
                if deps.get(k, 0) < v:
                    deps[k] = v
            if b.excl:
                for k, v in b.rd.items():
                    if deps.get(k, 0) < v:
                        deps[k] = v
        for b in writes:
            if b.lw is not None:
                k, v = b.lw
                if deps.get(k, 0) < v:
                    deps[k] = v
            for k, v in b.rd.items():
                if deps.get(k, 0) < v:
                    deps[k] = v
        return deps

    def _mark(self, key, val, reads, writes):
        for b in reads:
            if b.rd.get(key, 0) < val:
                b.rd[key] = val
        for b in writes:
            b.lw = (key, val)
            b.rd = {}

    def op(self, eng, fn, reads=(), writes=()):
        deps = self._deps(reads, writes)
        waits = self._need(eng, deps)
        self.cnt[eng] += 1
        val = self.cnt[eng]
        key = ("c", eng)
        self.lists[eng].append((waits, fn, key, 1))
        self._mark(key, val, reads, writes)
        self.nins += 1

    def dma(self, q, fn, reads=(), writes=()):
        deps = self._deps(reads, writes)
        i = self.dma_i[q]
        self.dma_i[q] += 1
        j = i % self.NDMA
        key = ("d", q, j)
        self.dma_used.add(key)
        val = 16 * (i // self.NDMA + 1)
        if val > 16:
            deps[key] = max(deps.get(key, 0), val - 16)
        waits = self._need(q, deps)
        self.lists[q].append((waits, fn, key, 16))
        self._mark(key, val, reads, writes)
        self.nins += 1

    def barrier(self):
        allk = {}
        for e in self.ENG:
            if self.cnt[e] > 0:
                allk[("c", e)] = self.cnt[e]
        for q, n in self.dma_i.items():
            for j in range(min(n, self.NDMA)):
                allk[("d", q, j)] = 16 * ((n - j + self.NDMA - 1) // self.NDMA)
        for e in self.ENG:
            waits = self._need(e, dict(allk))
            if waits:
                self.lists[e].append((waits, None, None, 0))

    def replay(self):
        nc = self.nc
        with contextlib.ExitStack() as st:
            sems = {}
            for e in self.ENG:
                sems[("c", e)] = st.enter_context(nc.semaphore("c_" + e))
            for k in sorted(self.dma_used):
                sems[k] = st.enter_context(nc.semaphore("d_%s_%d" % (k[1], k[2])))
            block = st.enter_context(nc.Block())

            def run(lst):
                def f(eng):
                    for waits, fn, key, inc in lst:
                        for k, v in waits:
                            eng.wait_ge(sems[k], v)
                        if fn is not None:
                            fn(eng).then_inc(sems[key], inc)
                return f
            block.tensor(run(self.lists["pe"]))
            block.scalar(run(self.lists["act"]))
            block.vector(run(self.lists["dve"]))
            block.gpsimd(run(self.lists["pool"]))
            block.sync(run(self.lists["sp"]))


def make_consts():
    j = np.arange(128)[:, None]
    i = np.arange(128)[None, :]
    blocks = []
    blocks.append((j == i).astype(np.float32))
    pm = np.arange(128) % 64
    partner = np.where(pm < 8, np.arange(128) + 8, np.where(pm < 16, np.arange(128) - 8, -1))
    Rm = np.zeros((128, 128), np.float32)
    for pp in range(128):
        if partner[pp] >= 0:
            Rm[partner[pp], pp] = 1.0
    blocks.append(Rm)
    negA = np.where(j >= i, 0.0, NEGBIG).astype(np.float32)
    blocks.append(negA)
    blocks.append(np.where(j < 64, NEGBIG, negA).astype(np.float32))
    negB = np.where(j <= i, 0.0, NEGBIG).astype(np.float32)
    blocks.append(negB)
    blocks.append(np.where(j >= 64, NEGBIG, negB).astype(np.float32))
    blocks.append((j <= i).astype(np.float32))
    blocks.append((j >= i).astype(np.float32))
    blocks.append(np.broadcast_to((j == 127), (128, 128)).astype(np.float32))
    blocks.append(np.broadcast_to((j == 0), (128, 128)).astype(np.float32))
    blocks.append(np.where(j <= i, 0.0, NEGBIG).astype(np.float32))
    blocks.append(np.where(j >= i, 0.0, NEGBIG).astype(np.float32))
    blocks.append(np.ones((128, 128), np.float32))
    cpack = np.concatenate(blocks, axis=1)
    inv = (500000.0 ** (-np.arange(0, 16, 2, dtype=np.float32) / 16)).astype(np.float32)
    ang = np.arange(SMAX, dtype=np.float32)[None, :] * inv[:, None]
    cosT = np.ones((128, SMAX), np.float32)
    sinT = np.zeros((128, SMAX), np.float32)
    for pp in range(128):
        m = pp % 64
        if m < 8:
            cosT[pp] = np.cos(ang[m]); sinT[pp] = -np.sin(ang[m])
        elif m < 16:
            cosT[pp] = np.cos(ang[m - 8]); sinT[pp] = np.sin(ang[m - 8])
    sel32 = np.zeros((32, 32, 128), np.float32)
    for k in range(32):
        sel32[k, k, :] = 1.0
    return cpack, cosT, sinT, sel32.reshape(32, 32 * 128)


NCB = 13


def lam_init(li):
    return 0.8 - 0.6 * math.exp(-0.3 * li)


SP_GMIX, SP_GFFN, SP_GPLE, SP_GSSM, SP_CONVW, SP_CONVB, SP_GSUB, SP_PER_LAYER = 0, 8, 16, 24, 32, 92, 104, 105
RP_LAM, RP_DTB, RP_ALOG, RP_SSMD, RP_PER_LAYER = 0, 256, 288, 320, 336
RP_FIN = RP_PER_LAYER * DEPTH
RP_TOT = RP_FIN + D


def make_packs(inp):
    sp = np.zeros((128, SP_PER_LAYER * DEPTH), np.float32)
    rp = np.zeros((1, RP_TOT), np.float32)
    for l in range(DEPTH):
        o = l * SP_PER_LAYER
        sp[:, o + SP_GMIX:o + SP_GMIX + 8] = inp["norm_mix_g"][l].reshape(8, 128).T
        sp[:, o + SP_GFFN:o + SP_GFFN + 8] = inp["norm_ffn_g"][l].reshape(8, 128).T
        sp[:, o + SP_GPLE:o + SP_GPLE + 8] = inp["norm_ple_g"][l].reshape(8, 128).T
        sp[:, o + SP_GSSM:o + SP_GSSM + 8] = inp["ssm_norm_g"][l].reshape(8, 128).T
        cw = inp["conv_w"][l].reshape(5, 12, 128)
        sp[:, o + SP_CONVW:o + SP_CONVW + 60] = cw.transpose(2, 1, 0).reshape(128, 60)
        sp[:, o + SP_CONVB:o + SP_CONVB + 12] = inp["conv_b"][l].reshape(12, 128).T
        sp[:, o + SP_GSUB] = inp["diff_subln_g"][l]
        r = l * RP_PER_LAYER
        rp[0, r + RP_LAM:r + RP_LAM + 256] = inp["diff_lambda"][l].reshape(-1)
        rp[0, r + RP_DTB:r + RP_DTB + 32] = inp["dt_bias"][l].reshape(-1)
        rp[0, r + RP_ALOG:r + RP_ALOG + 32] = inp["a_log"][l].reshape(-1)
        rp[0, r + RP_SSMD:r + RP_SSMD + 16] = inp["ssm_d"][l].reshape(-1)
    rp[0, RP_FIN:] = inp["final_norm_g"]
    return sp, rp


WEIGHTS = (("w_in", D, WIN), ("w_br_a", 512, D), ("w_br_b", 512, D), ("w_br_c", D, D), ("w_out", D, D),
           ("w_ffn_in", D, 2 * FFN), ("w_ffn_out", FFN, D), ("w_ple_gate", D, D), ("w_ple_proj", 256, D))


class Builder:
    AW = 47 * 1024

    def __init__(self, seqs, dbg=False, stages=None, nlayers=DEPTH):
        self.nlayers = nlayers
        self.seqs = list(seqs)
        self.TOK = sum(seqs)
        self.dbg = dbg
        self.stages = stages
        self.nc = nc = bass.Bass("TRN2", target_bir_lowering=False)
        self.em = Em(nc)
        TOK = self.TOK
        skind = "ExternalOutput" if dbg else "Internal"

        def dram(name, shape, dt, kind):
            return nc.dram_tensor(name, list(shape), dt, kind=kind).ap()
        self.x = dram("x", [TOK, D], F32, "ExternalInput")
        self.p = dram("p", [DEPTH, TOK, 256], F32, "ExternalInput")
        self.cpack = dram("cpack", [128, NCB * 128], F32, "ExternalInput")
        self.cosd = dram("cosT", [128, SMAX], F32, "ExternalInput")
        self.sind = dram("sinT", [128, SMAX], F32, "ExternalInput")
        self.sel32d = dram("sel32", [32, 32 * 128], F32, "ExternalInput")
        self.spd = dram("spack", [128, SP_PER_LAYER * DEPTH], F32, "ExternalInput")
        self.rpd = dram("rpack", [1, RP_TOT], F32, "ExternalInput")
        self.w = {}
        self.wb = {}
        for name, k, n in WEIGHTS:
            self.w[name] = dram(name, [DEPTH, k, n], F32, "ExternalInput")
            self.wb[name] = dram("b_" + name, [DEPTH, k, n], BF16, "Internal")
        self.y = dram("y", [TOK, D], F32, "ExternalOutput")
        self.xs1 = dram("xs1", [TOK, D], F32, skind)
        self.oT = dram("oT", [2048, TOK], BF16, skind)
        self.gT = dram("gT", [3072, TOK], BF16, skind)
        self.bpart = dram("bpart", [3, TOK, 520], F32, skind)
        self.xstok = dram("xstok", [TOK, 1024], BF16, skind)
        self.btok = dram("btok", [TOK, 256], BF16, skind)
        self.bct = dram("bct", [512, TOK], BF16, skind)
        self.szd = dram("szd", [TOK, 1024], BF16, skind)
        self.hbd = dram("hbd", [2, TOK // 128, 128, 512], BF16, skind)

    def alloc(self, n_free, dt, parts=128):
        words = n_free if dt == F32 else (n_free + 1) // 2
        assert self.top + words <= self.AW, ("arena overflow", self.top, words)
        v = self.arena[0:parts, self.top:self.top + words]
        self.top += words
        if dt != F32:
            v = v.bitcast(dt)[:, 0:n_free]
        return v

    def alloc3(self, a, b, dt):
        return self.alloc(a * b, dt).rearrange("p (a b) -> p a b", a=a)

    _uid = 0

    def nb(self, tag="b"):
        Builder._uid += 1
        return Buf((tag, Builder._uid))

    def ring(self, n, n_free, dt, tag="r"):
        return [(self.alloc(n_free, dt), self.nb(tag)) for _ in range(n)]

    def db(self, *key):
        return self.em.buf(key)

    def wslab(self, name, l, c0, ncols, kc_n, r0=0):
        return self.wb[name][l, r0:r0 + kc_n * 128, c0:c0 + ncols].rearrange("(kc p) n -> p kc n", p=128)

    def build(self):
        nc, em = self.nc, self.em
        with contextlib.ExitStack() as st:
            self.arena = st.enter_context(nc.sbuf_tensor("arena", [128, self.AW], F32))
            self.top = 0
            self.ps = []
            self.psb = []
            for i in range(8):
                t = st.enter_context(nc.psum_tensor("ps%d" % i, [128, 512], F32))
                self.ps.append(t[:, :])
                self.psb.append(Buf(("ps", i), excl=True))
            self.setup()
            tok0 = 0
            for si, S in enumerate(self.seqs):
                self.run_seq(si, tok0, S)
                tok0 += S
            em.barrier()
            em.replay()
        return nc

    def setup(self):
        em, nc = self.em, self.nc
        self.cpf = self.alloc(NCB * 128, F32)
        self.cpb = self.alloc(NCB * 128, BF16)
        self.cosb = self.alloc(SMAX, BF16)
        self.sinb = self.alloc(SMAX, BF16)
        self.spf = self.alloc(SP_PER_LAYER * DEPTH, F32)
        self.rpf = self.alloc(RP_TOT, F32)
        self.neglam = self.alloc(DEPTH, F32)
        self.gsub = self.alloc(DEPTH, F32)
        self.aneg = self.alloc(32 * DEPTH, F32)
        cb = self.nb("const")
        em.dma("sp", lambda e: e.dma_start(self.cpf, self.cpack[:, :]), writes=[cb])
        em.dma("sp", lambda e: e.dma_start(self.spf, self.spd[:, :]), writes=[cb])
        em.dma("sp", lambda e: e.dma_start(self.rpf, self.rpd[0:1, :].broadcast_to([128, RP_TOT])), writes=[cb])
        em.dma("pool", lambda e: e.dma_start(self.cosb, self.cosd[:, :]), writes=[cb])
        em.dma("pool", lambda e: e.dma_start(self.sinb, self.sind[:, :]), writes=[cb])
        em.op("dve", lambda e: e.tensor_copy(self.cpb, self.cpf), reads=[cb], writes=[cb])
        import os
        SK = os.environ.get('SKIP', '')
        for name, k, n in (WEIGHTS if 'p' not in SK else ()):
            for l in range(DEPTH):
                for r in range(0, k, 128):
                    rr = min(128, k - r)
                    em.dma("pool", lambda e, name=name, l=l, r=r, rr=rr: e.dma_start(self.wb[name][l, r:r + rr, :], self.w[name][l, r:r + rr, :]))
        tmp = self.alloc(256, F32)
        t2 = self.alloc(4, F32)
        for l in (range(DEPTH) if 's' not in SK else ()):
            r = l * RP_PER_LAYER
            lp = self.rpf[:, r + RP_LAM:r + RP_LAM + 256].rearrange("p (a d) -> p a d", a=4)
            pr = tmp[:, 0:128].rearrange("p (a d) -> p a d", a=2)
            em.op("dve", lambda e, lp=lp, pr=pr: e.tensor_tensor(pr, lp[:, 0:4:2, :], lp[:, 1:4:2, :], ALU.mult), reads=[cb], writes=[cb])
            em.op("dve", lambda e, pr=pr: e.tensor_reduce(t2[:, 0:2], pr, mybir.AxisListType.X, ALU.add), reads=[cb], writes=[cb])
            em.op("act", lambda e: e.activation(t2[:, 2:4], t2[:, 0:2], AF.Exp), reads=[cb], writes=[cb])
            em.op("dve", lambda e, l=l: e.scalar_tensor_tensor(self.neglam[:, l:l + 1], t2[:, 3:4], -lam_init(l), t2[:, 2:3], ALU.add, ALU.subtract), reads=[cb], writes=[cb])
            em.op("dve", lambda e, l=l: e.tensor_scalar(self.gsub[:, l:l + 1], self.spf[:, l * SP_PER_LAYER + SP_GSUB:l * SP_PER_LAYER + SP_GSUB + 1], 1.0 - lam_init(l), None, ALU.mult), reads=[cb], writes=[cb])
            em.op("act", lambda e, l=l, r=r: e.activation(self.aneg[:, l * 32:(l + 1) * 32], self.rpf[:, r + RP_ALOG:r + RP_ALOG + 32], AF.Exp), reads=[cb], writes=[cb])
            em.op("dve", lambda e, l=l: e.tensor_scalar(self.aneg[:, l * 32:(l + 1) * 32], self.aneg[:, l * 32:(l + 1) * 32], -1.0, None, ALU.mult), reads=[cb], writes=[cb])
        self.cb = cb
        self.n_junk = self.alloc(D, BF16)
        self.n_xn = self.ring(2, D, BF16, "xn")
        self.n_st = self.ring(2, 4, F32, "nst")
        self.n_i = 0
        self.xt = self.ring(2, D, F32, "xt")
        em.barrier()
        self.base_top = self.top

    def cblk(self, i, bf=True):
        t = self.cpb if bf else self.cpf
        return t[:, i * 128:(i + 1) * 128]

    def norm_tile(self, xt, xtb, gcol, dst, dstb, pbanks=(6, 7)):
        em = self.em
        i = self.n_i
        self.n_i += 1
        xn, xnb = self.n_xn[i % 2]
        stt, stb = self.n_st[i % 2]
        junk = self.n_junk
        jb = self.db("junk")
        em.op("act", lambda e: e.activation(junk, xt, AF.Square, accum_out=stt[:, 0:1]), reads=[xtb], writes=[jb, stb])
        em.op("dve", lambda e: e.tensor_scalar(stt[:, 1:2], stt[:, 0:1], 1.0 / D, EPS, ALU.mult, ALU.add), reads=[stb], writes=[stb])
        em.op("act", lambda e: e.activation(stt[:, 2:3], stt[:, 1:2], AF.Sqrt), reads=[stb], writes=[stb])
        em.op("dve", lambda e: e.reciprocal(stt[:, 3:4], stt[:, 2:3]), reads=[stb], writes=[stb])
        em.op("dve", lambda e: e.tensor_scalar(xn, xt, stt[:, 3:4], None, ALU.mult), reads=[xtb, stb], writes=[xnb])
        for half in range(2):
            bi = pbanks[half]
            pv = self.ps[bi].bitcast(BF16)[:, 0:512]
            for c in range(4):
                cc = half * 4 + c
                em.op("pe", lambda e, c=c, cc=cc, pv=pv: e.transpose(pv[:, c * 128:(c + 1) * 128], xn[:, cc * 128:(cc + 1) * 128], self.cblk(0)), reads=[xnb, self.cb], writes=[self.psb[bi]])
            g4 = gcol[:, half * 4:half * 4 + 4].unsqueeze(2).broadcast_to([128, 4, 128])
            em.op("dve", lambda e, pv=pv, half=half, g4=g4: e.tensor_tensor(dst[:, half * 4:half * 4 + 4, :], pv.rearrange("p (c n) -> p c n", c=4), g4, ALU.mult), reads=[self.psb[bi], self.cb], writes=[dstb])

    def run_seq(self, si, tok0, S):
        em = self.em
        NB = S // 512
        for l in range(self.nlayers):
            self.top = self.base_top
            st = self.stages
            NT = S // 128
            self.dt_ = self.alloc(NT * 32, F32).rearrange("p (t c) -> p t c", c=32)
            hT = self.alloc(8 * S, BF16).rearrange("p (c t) -> p c t", c=8)
            hTb = [self.nb("hT") for _ in range(NB)]
            import os
            if 'n' not in os.environ.get('SKIP', ''):
                self.stage_norm(si, tok0, S, l, hT, hTb)
            m0 = self.top
            if st is None or "A" in st:
                self.stage_A(si, tok0, S, l, hT, hTb)
                em.barrier()
            self.top = m0
            if st is None or "B" in st:
                self.stage_B(si, tok0, S, l, hT, hTb)
                em.barrier()
            self.top = m0
            if st is None or "C" in st:
                self.stage_Cprep(si, tok0, S, l, hT, hTb)
                em.barrier()
            self.top = m0
            if st is None or "G" in st:
                self.stage_G(si, tok0, S, l, hT, hTb)
                em.barrier()
            self.top = m0 - (8 * S) // 2
            m1 = self.top
            if st is None or "B" in st:
                self.stage_Bmerge(si, tok0, S, l)
                em.barrier()
            self.top = m1
            if st is None or "C" in st:
                self.stage_Cscan(si, tok0, S, l)
                em.barrier()
            self.top = self.base_top
            if st is None or "M" in st:
                self.stage_M(si, tok0, S, l)
                em.barrier()

    def stage_norm(self, si, tok0, S, l, hT, hTb):
        em = self.em
        src = self.x if l == 0 else self.xs1
        gcol = self.spf[:, l * SP_PER_LAYER + SP_GMIX:l * SP_PER_LAYER + SP_GMIX + 8]
        for tt in range(S // 128):
            xt, xtb = self.xt[tt % 2]
            t0 = tok0 + tt * 128
            rd = [self.db("xs1", t0 // 512)] if l > 0 else []
            em.dma("sp", lambda e, xt=xt, t0=t0: e.dma_start(xt, src[t0:t0 + 128, :]), reads=rd, writes=[xtb])
            self.norm_tile(xt, xtb, gcol, hT[:, :, tt * 128:(tt + 1) * 128], hTb[tt // 4])

    def rope_block(self, bi, bi2, view, dst, dstb, cview, sview):
        em = self.em
        k = self.rope_i
        self.rope_i += 1
        ub, ubb = self.r_ub[k % 2]
        t1, t1b = self.r_t1[k % 2]
        t2, t2b = self.r_t2[k % 2]
        em.op("act", lambda e: e.copy(ub, self.ps[bi]), reads=[self.psb[bi]], writes=[ubb])
        em.op("pe", lambda e: e.matmul(self.ps[bi2], self.cblk(1), ub, start=True, stop=True), reads=[ubb, self.cb], writes=[self.psb[bi2]])
        em.op("dve", lambda e: e.tensor_tensor(view(t1), view(self.ps[bi]), cview, ALU.mult), reads=[self.psb[bi], self.cb], writes=[t1b])
        em.op("dve", lambda e: e.tensor_tensor(view(t2), view(self.ps[bi2]), sview, ALU.mult), reads=[self.psb[bi2], self.cb], writes=[t2b])
        em.op(os.environ.get("ROPE_ENG", "pool"), lambda e: e.tensor_tensor(dst, view(t1), view(t2), ALU.add), reads=[t1b, t2b], writes=[dstb])

    def rope_alloc(self):
        self.r_ub = self.ring(2, 512, BF16, "ub")
        self.r_t1 = self.ring(2, 512, F32, "t1")
        self.r_t2 = self.ring(2, 512, F32, "t2")
        self.rope_i = 0

    def stage_A(self, si, tok0, S, l, hT, hTb):
        em = self.em
        NT, NB = S // 128, S // 512
        ps, psb = self.ps, self.psb
        self.rope_alloc()
        v_all = self.alloc3(NT, 128, BF16)
        vb = [self.nb("v") for _ in range(NT)]
        wvr = self.ring(2, 8 * 128, BF16, "wv")

        def vproj(hh):
            wvt, wvb = wvr[hh % 2]
            wv = wvt.rearrange("p (k n) -> p k n", k=8)
            em.dma("sp", lambda e: e.dma_start(wv, self.wslab("w_in", l, OFF_AV + hh * 128, 128, 8)), writes=[wvb])
            for tt in range(NT):
                bi = tt % 2
                for kc in range(8):
                    em.op("pe", lambda e, tt=tt, kc=kc, bi=bi: e.matmul(ps[bi][:, 0:128], hT[:, kc, tt * 128:(tt + 1) * 128], wv[:, kc, :], start=(kc == 0), stop=(kc == 7)),
                          reads=[hTb[tt // 4], wvb], writes=[psb[bi]])
                em.op("act", lambda e, tt=tt, bi=bi: e.copy(v_all[:, tt, :], ps[bi][:, 0:128]), reads=[psb[bi]], writes=[vb[tt]])
        import os
        ACUT = 9
        wqk = [(self.alloc3(8, 128, BF16), self.nb("wqk")) for _ in range(4)]
        qT = self.alloc(S, BF16)
        kT = self.alloc(S, BF16)
        qTb = [self.nb("qT") for _ in range(NB)]
        kTb = [self.nb("kT") for _ in range(NB)]
        pT = self.ring(4, 512, BF16, "pT")
        ep = self.ring(6, 512, F32, "ep")
        sq = self.ring(1, 512, BF16, "sq")[0]
        oaT = self.ring(2, 512, BF16, "oaT")
        ones_b = self.cblk(12)
        pti = 0
        for hh in range(4):
            vproj(hh)
            for wi, (off, dst, dstb) in enumerate(((OFF_AQ, qT, qTb), (OFF_AK, kT, kTb))):
                wt, wtb = wqk[(hh * 2 + wi) % 4]
                em.dma("sp", lambda e, wt=wt, off=off, hh=hh: e.dma_start(wt, self.wslab("w_in", l, off + hh * 128, 128, 8)), writes=[wtb])
                for blk in range(NB):
                    bi = blk % 2
                    for kc in range(8):
                        em.op("pe", lambda e, wt=wt, kc=kc, blk=blk, bi=bi: e.matmul(ps[bi], wt[:, kc, :], hT[:, kc, blk * 512:(blk + 1) * 512], start=(kc == 0), stop=(kc == 7)),
                              reads=[wtb, hTb[blk]], writes=[psb[bi]])
                    sl = slice(blk * 512, (blk + 1) * 512)
                    self.rope_block(bi, 2 + bi, lambda a: a, dst[:, sl], dstb[blk], self.cosb[:, sl], self.sinb[:, sl])
            for qb in range(NB):
                steps = [(kt, c) for kt in range(NT) for c in range(2)]

                def issue_score(s_, qb=qb):
                    kt, c = steps[s_]
                    sb = s_ % 4
                    hs = slice(c * 64, (c + 1) * 64)
                    em.op("pe", lambda e: e.matmul(ps[sb], kT[hs, kt * 128:(kt + 1) * 128], qT[hs, qb * 512:(qb + 1) * 512], start=True, stop=True),
                          reads=[kTb[kt // 4], qTb[qb]], writes=[psb[sb]])
                    pt, ptb = pT[s_ % 4]
                    em.op("act", lambda e: e.activation(pt, ps[sb], AF.Exp, scale=0.125), reads=[psb[sb]], writes=[ptb])

                def issue_av(s_):
                    kt, c = steps[s_]
                    pt, ptb = pT[s_ % 4]
                    em.op("pe", lambda e: e.matmul(ps[4 + 2 * c], v_all[:, kt, :], pt, start=(kt == 0), stop=(kt == NT - 1)),
                          reads=[ptb, vb[kt]], writes=[psb[4 + 2 * c]])
                    em.op("pe", lambda e: e.matmul(ps[5 + 2 * c], ones_b, pt, start=(kt == 0), stop=(kt == NT - 1)),
                          reads=[ptb, self.cb], writes=[psb[5 + 2 * c]])
                issue_score(0)
                issue_score(1)
                for kt in range(NT):
                    if kt + 1 < NT:
                        issue_score(2 * kt + 2)
                        issue_score(2 * kt + 3)
                    issue_av(2 * kt)
                    issue_av(2 * kt + 1)
                (r0, r0b), (r1, r1b), (t0, t0b), (t1, t1b), (o, ob), (rs, rsb) = ep
                em.op("dve", lambda e: e.reciprocal(r0, ps[5]), reads=[psb[5]], writes=[r0b])
                em.op("dve", lambda e: e.reciprocal(r1, ps[7]), reads=[psb[7]], writes=[r1b])
                em.op("dve", lambda e: e.tensor_tensor(t0, ps[4], r0, ALU.mult), reads=[psb[4], r0b], writes=[t0b])
                em.op("dve", lambda e: e.tensor_tensor(t1, ps[6], r1, ALU.mult), reads=[psb[6], r1b], writes=[t1b])
                em.op("dve", lambda e: e.scalar_tensor_tensor(o, t1, self.neglam[:, l:l + 1], t0, ALU.mult, ALU.add), reads=[t0b, t1b, self.cb], writes=[ob])
                em.op("act", lambda e: e.activation(sq[0], o, AF.Square), reads=[ob], writes=[sq[1]])
                em.op("pe", lambda e: e.matmul(ps[5], ones_b, sq[0], start=True, stop=True), reads=[sq[1], self.cb], writes=[psb[5]])
                em.op("dve", lambda e: e.tensor_scalar(rs, ps[5], 1.0 / 128, EPS, ALU.mult, ALU.add), reads=[psb[5]], writes=[rsb])
                em.op("act", lambda e: e.activation(rs, rs, AF.Sqrt), reads=[rsb], writes=[rsb])
                em.op("dve", lambda e: e.reciprocal(rs, rs), reads=[rsb], writes=[rsb])
                em.op("dve", lambda e: e.tensor_tensor(t0, o, rs, ALU.mult), reads=[ob, rsb], writes=[t0b])
                oa, oab = oaT[(hh * NB + qb) % 2]
                em.op("dve", lambda e, oa=oa: e.tensor_scalar(oa, t0, self.gsub[:, l:l + 1], None, ALU.mult), reads=[t0b, self.cb], writes=[oab])
                g0 = tok0 + qb * 512
                em.dma("pool", lambda e, oa=oa, hh=hh, g0=g0: e.dma_start(self.oT[hh * 128:(hh + 1) * 128, g0:g0 + 512], oa),
                       reads=[oab], writes=[self.db("oT", g0 // 512, "a%d" % hh)])

    def stage_B(self, si, tok0, S, l, hT, hTb):
        em = self.em
        NT, NB = S // 128, S // 512
        ps, psb = self.ps, self.psb
        self.rope_alloc()
        ident_b = self.cblk(0)
        wq = self.ring(2, 8 * 128, BF16, "wq")
        wk = self.ring(2, 8 * 128, BF16, "wk")
        wv = self.ring(2, 8 * 128, BF16, "wv")
        qT = self.alloc(S, BF16)
        qTb = self.nb("qTc")
        kTp = self.alloc(S + 16 * 128, BF16)
        kTb = self.nb("kTp")
        vaug = self.alloc((NT + 16) * 130, BF16)
        vab = self.nb("vaug")
        pA = self.ring(2, 512, BF16, "pA")
        pB = self.ring(2, 512, BF16, "pB")
        stg = self.ring(2, 4 * 130, F32, "stg")
        state = {'it': 0}

        def group(g, dil):
            L = S // dil
            nt_c = L // 128
            ntile = dil * (nt_c + 1)
            va = vaug[:, 0:ntile * 130].rearrange("p (t h e) -> p t h e", h=2, e=65)
            kp = kTp[:, 0:dil * (L + 128)].rearrange("p (r i) -> p r i", r=dil)
            q3 = qT[:, 0:S].rearrange("p (r i) -> p r i", r=dil)
            hTv = [hT[:, kc, :].rearrange("p (i d) -> p d i", d=dil) for kc in range(8)]
            cosv = self.cosb[:, 0:S].rearrange("p (i d) -> p d i", d=dil)
            sinv = self.sinb[:, 0:S].rearrange("p (i d) -> p d i", d=dil)
            nr = max(1, 512 // L)
            ni = min(512, L)
            view = lambda a, nr=nr: a.rearrange("p (r i) -> p r i", r=nr)

            def pair(hp):
                it = state['it']
                wqt, wqb = wq[it % 2]
                wkt, wkb = wk[it % 2]
                wvt, wvb = wv[it % 2]
                state['it'] += 1
                wq3 = wqt.rearrange("p (k n) -> p k n", k=8)
                wk3 = wkt.rearrange("p (k n) -> p k n", k=8)
                wv3 = wvt.rearrange("p (k n) -> p k n", k=8)
                cq = g * 512 + hp * 128
                em.dma("sp", lambda e, wq3=wq3, cq=cq: e.dma_start(wq3, self.wslab("w_in", l, OFF_BQ + cq, 128, 8)), writes=[wqb])
                em.dma("sp", lambda e, wk3=wk3, cq=cq: e.dma_start(wk3, self.wslab("w_in", l, OFF_BK + cq, 128, 8)), writes=[wkb])
                em.dma("sp", lambda e, wv3=wv3, cq=cq: e.dma_start(wv3, self.wslab("w_in", l, OFF_BV + cq, 128, 8)), writes=[wvb])
                em.op("pool", lambda e, kp=kp: e.memset(kp, 0.0), writes=[kTb])
                em.op("pool", lambda e, va=va: e.memset(va, 0.0), writes=[vab])
                em.op("pool", lambda e, va=va: e.memset(va[:, :, :, 64:65], 1.0), writes=[vab])
                BCUT = int(os.environ.get('BCUT', '9'))
                if BCUT < 1:
                    return
                for w3, wb_, isq in ((wq3, wqb, True), (wk3, wkb, False)):
                    for blk in range(NB):
                        r0 = (blk * 512) // L
                        i0 = (blk * 512) % L
                        bi = blk % 2
                        for kc in range(8):
                            em.op("pe", lambda e, w3=w3, kc=kc, bi=bi, r0=r0, i0=i0: e.matmul(view(ps[bi]), w3[:, kc, :], hTv[kc][:, r0:r0 + nr, i0:i0 + ni], start=(kc == 0), stop=(kc == 7)),
                                  reads=[wb_] + hTb, writes=[psb[bi]])
                        if isq:
                            dst, dstb = q3[:, r0:r0 + nr, i0:i0 + ni], qTb
                        else:
                            dst, dstb = kp[:, r0:r0 + nr, 64 + i0:64 + i0 + ni], kTb
                        self.rope_block(bi, 2 + bi, view, dst, dstb, cosv[:, r0:r0 + nr, i0:i0 + ni], sinv[:, r0:r0 + nr, i0:i0 + ni])
                if BCUT < 2:
                    return
                vi = 0
                for r in range(dil):
                    for a in range(nt_c + 1):
                        lo, hi = max(0, 128 * a - 64), min(L, 128 * a + 64)
                        p0 = lo - (128 * a - 64)
                        n = hi - lo
                        bi = 4 + vi % 2
                        vi += 1
                        for kc in range(8):
                            em.op("pe", lambda e, kc=kc, bi=bi, r=r, lo=lo, hi=hi, p0=p0, n=n: e.matmul(ps[bi][p0:p0 + n, 0:128], hTv[kc][:, r, lo:hi], wv3[:, kc, :], start=(kc == 0), stop=(kc == 7)),
                                  reads=[wvb] + hTb, writes=[psb[bi]])
                        ti = r * (nt_c + 1) + a
                        em.op("act", lambda e, bi=bi, p0=p0, n=n, ti=ti: e.copy(va[p0:p0 + n, ti, :, 0:64], ps[bi][p0:p0 + n, 0:128].rearrange("p (h e) -> p h e", h=2)),
                              reads=[psb[bi]], writes=[vab])
                if BCUT < 3:
                    return
                nq = min(4, nt_c)
                gi = 0
                for r in range(dil):
                    for qg in range(nt_c // nq):
                        sg, sgb = stg[gi % 2]
                        gi += 1
                        sg4 = sg.rearrange("p (j h e) -> p j h e", h=2, e=65)
                        for h in range(2):
                            hs = slice(h * 64, (h + 1) * 64)
                            ba, bb, bo = 0 + h, 2 + h, 6 + h
                            for jq in range(nq):
                                qi = qg * nq + jq
                                cs_ = slice(jq * 128, (jq + 1) * 128)
                                mA = self.cblk(3 if qi == 0 else 2)
                                mB = self.cblk(5 if qi == nt_c - 1 else 4)
                                em.op("pe", lambda e, ba=ba, cs_=cs_, hs=hs, r=r, qi=qi: e.matmul(ps[ba][:, cs_], kp[hs, r, 128 * qi:128 * qi + 128], q3[hs, r, 128 * qi:128 * qi + 128], start=True, stop=False),
                                      reads=[kTb, qTb], writes=[psb[ba]])
                                em.op("pe", lambda e, ba=ba, cs_=cs_, mA=mA: e.matmul(ps[ba][:, cs_], ident_b, mA, start=False, stop=True), reads=[self.cb], writes=[psb[ba]])
                                em.op("pe", lambda e, bb=bb, cs_=cs_, hs=hs, r=r, qi=qi: e.matmul(ps[bb][:, cs_], kp[hs, r, 128 * qi + 128:128 * qi + 256], q3[hs, r, 128 * qi:128 * qi + 128], start=True, stop=False),
                                      reads=[kTb, qTb], writes=[psb[bb]])
                                em.op("pe", lambda e, bb=bb, cs_=cs_, mB=mB: e.matmul(ps[bb][:, cs_], ident_b, mB, start=False, stop=True), reads=[self.cb], writes=[psb[bb]])
                            pa, pab = pA[h]
                            pb, pbb = pB[h]
                            w_ = nq * 128
                            em.op("act", lambda e, pa=pa, ba=ba, w_=w_: e.activation(pa[:, 0:w_], ps[ba][:, 0:w_], AF.Exp, scale=0.125), reads=[psb[ba]], writes=[pab])
                            em.op("act", lambda e, pb=pb, bb=bb, w_=w_: e.activation(pb[:, 0:w_], ps[bb][:, 0:w_], AF.Exp, scale=0.125), reads=[psb[bb]], writes=[pbb])
                        for h in range(2):
                            hs = slice(h * 64, (h + 1) * 64)
                            ba, bb, bo = 0 + h, 2 + h, 6 + h
                            pa, pab = pA[h]
                            pb, pbb = pB[h]
                            for jq in range(nq):
                                qi = qg * nq + jq
                                cs_ = slice(jq * 128, (jq + 1) * 128)
                                ti = r * (nt_c + 1) + qi
                                em.op("pe", lambda e, bo=bo, jq=jq, pa=pa, cs_=cs_, ti=ti, h=h: e.matmul(ps[bo][:, jq * 65:(jq + 1) * 65], pa[:, cs_], va[:, ti, h, :], start=True, stop=False),
                                      reads=[pab, vab], writes=[psb[bo]])
                                em.op("pe", lambda e, bo=bo, jq=jq, pb=pb, cs_=cs_, ti=ti, h=h: e.matmul(ps[bo][:, jq * 65:(jq + 1) * 65], pb[:, cs_], va[:, ti + 1, h, :], start=False, stop=True),
                                      reads=[pbb, vab], writes=[psb[bo]])
                            em.op("dve", lambda e, bo=bo, h=h, sg4=sg4: e.tensor_copy(sg4[:, 0:nq, h, :], ps[bo][:, 0:nq * 65].rearrange("p (j e) -> p j e", e=65)), reads=[psb[bo]], writes=[sgb])
                        base = tok0 + r + 128 * qg * nq * dil
                        dstv = self.bpart[g, base:base + (nq * 128 - 1) * dil + 1:dil, hp * 130:(hp + 1) * 130].rearrange("(j q) c -> q j c", q=128)
                        srcv = sg[:, 0:nq * 130].rearrange("p (j c) -> p j c", c=130)
                        em.dma("pool", lambda e, dstv=dstv, srcv=srcv: e.dma_start(dstv, srcv), reads=[sgb], writes=[self.db("bpart", si, l, g, hp, r, qg)])
            for hp in range(4):
                pair(hp)

        for g, dil in enumerate(DILS):
            group(g, dil)

    def stage_Bmerge(self, si, tok0, S, l):
        em = self.em
        if int(os.environ.get('BCUT', '9')) < 5:
            return
        ps, psb = self.ps, self.psb
        NB = S // 512
        part = self.ring(2, 3 * 4 * 520, F32, "part")
        acc = self.ring(2, 4 * 520, F32, "acc")
        rec = self.ring(2, 32, F32, "rec")
        obn = self.ring(2, 4 * 512, BF16, "obn")
        obT = self.ring(2, 4 * 512, BF16, "obT")
        for blk in range(NB):
            g0 = tok0 + blk * 512
            pt, ptb = part[blk % 2]
            ac, acb = acc[blk % 2]
            rc, rcb = rec[blk % 2]
            on, onb = obn[blk % 2]
            oT_, oTb_ = obT[blk % 2]
            p4 = pt.rearrange("p (g t c) -> p g t c", g=3, t=4)
            for g in range(3):
                em.dma("sp", lambda e, g=g, p4=p4, g0=g0: e.dma_start(p4[:, g, :, :], self.bpart[g, g0:g0 + 512, :].rearrange("(t q) c -> q t c", q=128)),
                       reads=[b for k, b in self.em.bufs.items() if k[0] == "bpart" and k[1] == si and k[2] == l and k[3] == g], writes=[ptb])
            MCUT = int(os.environ.get('MCUT', '9'))
            if MCUT < 2:
                continue
            a3 = ac.rearrange("p (t c) -> p t c", t=4)
            em.op("dve", lambda e, a3=a3, p4=p4: e.tensor_tensor(a3, p4[:, 0, :, :], p4[:, 1, :, :], ALU.add), reads=[ptb], writes=[acb])
            em.op("dve", lambda e, a3=a3, p4=p4: e.tensor_tensor(a3, a3, p4[:, 2, :, :], ALU.add), reads=[ptb, acb], writes=[acb])
            a4 = ac.rearrange("p (t h e) -> p t h e", t=4, e=65)
            r3 = rc.rearrange("p (t h) -> p t h", t=4)
            em.op("dve", lambda e, a4=a4, r3=r3: e.reciprocal(r3, a4[:, :, :, 64]), reads=[acb], writes=[rcb])
            o4 = on.rearrange("p (t h e) -> p t h e", t=4, e=64)
            em.op("dve", lambda e, a4=a4, r3=r3, o4=o4: e.tensor_tensor(o4, a4[:, :, :, 0:64], r3.unsqueeze(3).broadcast_to([128, 4, 8, 64]), ALU.mult), reads=[acb, rcb], writes=[onb])
            if MCUT < 3:
                continue
            o3 = on.rearrange("p (t f) -> p t f", t=4)
            oT3 = oT_.rearrange("p (j t) -> p j t", j=4)
            for j in range(4):
                bi = j
                pv = ps[bi].bitcast(BF16)[:, 0:512]
                for t in range(4):
                    em.op("pe", lambda e, pv=pv, t=t, j=j, o3=o3: e.transpose(pv[:, t * 128:(t + 1) * 128], o3[:, t, j * 128:(j + 1) * 128], self.cblk(0)), reads=[onb, self.cb], writes=[psb[bi]])
                em.op("dve", lambda e, pv=pv, j=j, oT3=oT3: e.tensor_copy(oT3[:, j, :], pv), reads=[psb[bi]], writes=[oTb_])
            em.dma("pool", lambda e, oT3=oT3, g0=g0: e.dma_start(self.oT[512:1024, g0:g0 + 512].rearrange("(j p) t -> p j t", p=128), oT3),
                   reads=[oTb_], writes=[self.db("oT", g0 // 512, "b")])

    def stage_G(self, si, tok0, S, l, hT, hTb):
        em = self.em
        ps, psb = self.ps, self.psb
        NB = S // 512
        wg = self.ring(2, 8 * 512, BF16, "wg")
        gt = self.ring(3, 512, BF16, "gt")
        k = 0
        for sl in range(6):
            wt, wtb = wg[sl % 2]
            w3 = wt.rearrange("p (k n) -> p k n", k=8)
            em.dma("sp", lambda e, w3=w3, sl=sl: e.dma_start(w3, self.wslab("w_in", l, OFF_G + sl * 512, 512, 8)), writes=[wtb])
            for sub in range(4):
                gc = sl * 4 + sub
                for blk in range(NB):
                    bi = k % 4
                    g_, gb_ = gt[k % 3]
                    k += 1
                    for kc in range(8):
                        em.op("pe", lambda e, w3=w3, kc=kc, sub=sub, blk=blk, bi=bi: e.matmul(ps[bi], w3[:, kc, sub * 128:(sub + 1) * 128], hT[:, kc, blk * 512:(blk + 1) * 512], start=(kc == 0), stop=(kc == 7)),
                              reads=[wtb, hTb[blk]], writes=[psb[bi]])
                    em.op("act", lambda e, g_=g_, bi=bi: e.activation(g_, ps[bi], AF.Sigmoid), reads=[psb[bi]], writes=[gb_])
                    g0 = tok0 + blk * 512
                    em.dma("pool", lambda e, g_=g_, gc=gc, g0=g0: e.dma_start(self.gT[gc * 128:(gc + 1) * 128, g0:g0 + 512], g_), reads=[gb_], writes=[self.db("gT", g0 // 512, gc)])

    def stage_M(self, si, tok0, S, l):
        self.top = self.base_top
        C = {}
        NX = 2
        C["xblk"] = [self.alloc3(4, D, F32) for _ in range(NX)]
        C["xbb"] = [[self.nb("xblk") for _ in range(4)] for _ in range(NX)]
        C["wring"] = self.ring(3, 8 * 512, BF16, "wm")
        C["tmpf"] = self.ring(3, 512, F32, "tmpf")
        C["hT2"] = self.alloc3(8, 512, BF16)
        C["hT2b"] = self.nb("hT2")
        C["mT"] = self.alloc3(8, 512, BF16)
        C["mTb"] = self.nb("mT")
        mark = self.top
        C["oTb_"] = self.alloc3(16, 512, BF16)
        C["oTbb"] = self.nb("oTblk")
        C["gTb_"] = self.alloc3(24, 512, BF16)
        C["gTbb"] = self.nb("gTblk")
        C["macc"] = self.alloc3(8, 512, F32)
        C["maccb"] = [self.nb("macc") for _ in range(8)]
        end1 = self.top
        self.top = mark
        C["aT"] = self.alloc3(22, 512, BF16)
        C["aTb"] = self.nb("aT")
        C["wfo"] = self.alloc3(22, 512, BF16)
        C["wfob"] = self.nb("wfo")
        C["pf"] = self.alloc3(4, 256, F32)
        C["pfb"] = self.nb("pf")
        C["pb16"] = self.alloc3(4, 256, BF16)
        C["pbb"] = self.nb("pb16")
        C["pT"] = self.alloc3(2, 512, BF16)
        C["pTb"] = self.nb("pT")
        C["yt"] = self.ring(2, D, F32, "yt")
        self.top = max(self.top, end1)
        C["P1"] = [C["oTbb"], C["gTbb"]] + C["maccb"]
        C["P2"] = [C["aTb"], C["wfob"], C["pfb"], C["pbb"], C["pTb"]] + [b_ for _, b_ in C["yt"]]
        C["wi"] = [0]
        C["ti"] = [0]
        for blk in range(S // 512):
            self.m_block(si, tok0 + blk * 512, l, C, blk)
        self.em.barrier()

    def m_block(self, si, g0, l, C, blk):
        em = self.em
        ps, psb = self.ps, self.psb
        last = (l == self.nlayers - 1)
        src = self.x if l == 0 else self.xs1
        gb = g0 // 512
        xblk = C["xblk"][blk % 2]
        xbb = C["xbb"][blk % 2]
        wring, tmpf = C["wring"], C["tmpf"]
        hT2, hT2b, mT, mTb = C["hT2"], C["hT2b"], C["mT"], C["mTb"]
        wi, ti = C["wi"], C["ti"]
        P1, P2 = C["P1"], C["P2"]

        def wload(name, c0, ncols, kc_n, r0=0):
            wt, wtb = wring[wi[0] % 3]
            wi[0] += 1
            w3 = wt[:, 0:kc_n * ncols].rearrange("p (k n) -> p k n", k=kc_n)
            em.dma("sp", lambda e: e.dma_start(w3, self.wslab(name, l, c0, ncols, kc_n, r0)), writes=[wtb])
            return w3, wtb

        def tmp():
            t = tmpf[ti[0] % 3]
            ti[0] += 1
            return t

        rd = [self.db("xs1", gb)] if l > 0 else []
        for t in range(4):
            em.dma("sp", lambda e, t=t: e.dma_start(xblk[:, t, :], src[g0 + t * 128:g0 + (t + 1) * 128, :]), reads=rd, writes=[xbb[t]])
        oTb_, oTbb, gTb_, gTbb, macc, maccb = C["oTb_"], C["oTbb"], C["gTb_"], C["gTbb"], C["macc"], C["maccb"]
        em.dma("sp", lambda e: e.dma_start(oTb_, self.oT[:, g0:g0 + 512].rearrange("(c p) t -> p c t", p=128)),
               reads=[b for k, b in em.bufs.items() if k[0] == "oT" and k[1] == gb], writes=[oTbb] + P2)
        em.dma("sp", lambda e: e.dma_start(gTb_, self.gT[:, g0:g0 + 512].rearrange("(c p) t -> p c t", p=128)),
               reads=[b for k, b in em.bufs.items() if k[0] == "gT" and k[1] == gb], writes=[gTbb] + P2)
        k = 0
        for br, (name, koff, kcn) in enumerate((("w_br_a", 0, 4), ("w_br_b", 4, 4), ("w_br_c", 8, 8))):
            for half in range(2):
                w3, wtb = wload(name, half * 512, 512, kcn)
                for sub in range(4):
                    n = half * 4 + sub
                    bi = k % 4
                    k += 1
                    for kc in range(kcn):
                        em.op("pe", lambda e, w3=w3, kc=kc, sub=sub, bi=bi, koff=koff, kcn=kcn: e.matmul(ps[bi], w3[:, kc, sub * 128:(sub + 1) * 128], oTb_[:, koff + kc, :], start=(kc == 0), stop=(kc == kcn - 1)),
                              reads=[wtb, oTbb], writes=[psb[bi]])
                    if br == 0:
                        em.op("dve", lambda e, n=n, bi=bi: e.tensor_tensor(macc[:, n, :], ps[bi], gTb_[:, n, :], ALU.mult), reads=[psb[bi], gTbb], writes=[maccb[n]])
                    else:
                        t_, tb_ = tmp()
                        em.op("dve", lambda e, n=n, bi=bi, br=br, t_=t_: e.tensor_tensor(t_, ps[bi], gTb_[:, br * 8 + n, :], ALU.mult), reads=[psb[bi], gTbb], writes=[tb_])
                        em.op("pool", lambda e, n=n, t_=t_: e.tensor_tensor(macc[:, n, :], macc[:, n, :], t_, ALU.add), reads=[tb_, maccb[n]], writes=[maccb[n]])
        for n in range(8):
            em.op("act", lambda e, n=n: e.copy(mT[:, n, :], macc[:, n, :]), reads=[maccb[n]], writes=[mTb])
        for half in range(2):
            w3, wtb = wload("w_out", half * 512, 512, 8)
            for t in range(4):
                bi = 4 + (half * 4 + t) % 2
                for kc in range(8):
                    em.op("pe", lambda e, w3=w3, kc=kc, t=t, bi=bi: e.matmul(ps[bi], mT[:, kc, t * 128:(t + 1) * 128], w3[:, kc, :], start=(kc == 0), stop=(kc == 7)),
                          reads=[wtb, mTb], writes=[psb[bi]])
                em.op("dve", lambda e, t=t, half=half, bi=bi: e.tensor_tensor(xblk[:, t, half * 512:(half + 1) * 512], xblk[:, t, half * 512:(half + 1) * 512], ps[bi], ALU.add),
                      reads=[psb[bi], xbb[t]], writes=[xbb[t]])
        o = l * SP_PER_LAYER
        for t in range(4):
            self.norm_tile(xblk[:, t, :], xbb[t], self.spf[:, o + SP_GFFN:o + SP_GFFN + 8], hT2[:, :, t * 128:(t + 1) * 128], hT2b)
        aT, aTb = C["aT"], C["aTb"]
        k = 0
        for js in range(6):
            ncols = min(512, FFN - js * 512)
            wg3, wgb = wload("w_ffn_in", js * 512, ncols, 8)
            wu3, wub = wload("w_ffn_in", FFN + js * 512, ncols, 8)
            for sub in range(ncols // 128):
                j = js * 4 + sub
                bg, bu = (k % 2) * 2, (k % 2) * 2 + 1
                k += 1
                for kc in range(8):
                    em.op("pe", lambda e, wg3=wg3, kc=kc, sub=sub, bg=bg: e.matmul(ps[bg], wg3[:, kc, sub * 128:(sub + 1) * 128], hT2[:, kc, :], start=(kc == 0), stop=(kc == 7)), reads=[wgb, hT2b], writes=[psb[bg]])
                for kc in range(8):
                    em.op("pe", lambda e, wu3=wu3, kc=kc, sub=sub, bu=bu: e.matmul(ps[bu], wu3[:, kc, sub * 128:(sub + 1) * 128], hT2[:, kc, :], start=(kc == 0), stop=(kc == 7)), reads=[wub, hT2b], writes=[psb[bu]])
                t_, tb_ = tmp()
                em.op("act", lambda e, t_=t_, bg=bg: e.activation(t_, ps[bg], AF.Silu), reads=[psb[bg]], writes=[tb_])
                em.op("dve", lambda e, t_=t_, bu=bu, j=j: e.tensor_tensor(aT[:, j, :], t_, ps[bu], ALU.mult), reads=[tb_, psb[bu]], writes=[aTb] + P1)
        wfo, wfob = C["wfo"], C["wfob"]
        for half in range(2):
            em.dma("sp", lambda e, half=half: e.dma_start(wfo, self.wslab("w_ffn_out", l, half * 512, 512, 22)), writes=[wfob] + P1)
            for t in range(4):
                bi = 4 + (half * 4 + t) % 2
                for j in range(22):
                    em.op("pe", lambda e, j=j, t=t, bi=bi: e.matmul(ps[bi], aT[:, j, t * 128:(t + 1) * 128], wfo[:, j, :], start=(j == 0), stop=(j == 21)), reads=[wfob, aTb], writes=[psb[bi]])
                em.op("dve", lambda e, t=t, half=half, bi=bi: e.tensor_tensor(xblk[:, t, half * 512:(half + 1) * 512], xblk[:, t, half * 512:(half + 1) * 512], ps[bi], ALU.add),
                      reads=[psb[bi], xbb[t]], writes=[xbb[t]])
        for t in range(4):
            self.norm_tile(xblk[:, t, :], xbb[t], self.spf[:, o + SP_GPLE:o + SP_GPLE + 8], hT2[:, :, t * 128:(t + 1) * 128], hT2b)
        pf, pfb, pb16, pbb, pT, pTb = C["pf"], C["pfb"], C["pb16"], C["pbb"], C["pT"], C["pTb"]
        em.dma("sp", lambda e: e.dma_start(pf, self.p[l, g0:g0 + 512, :].rearrange("(t q) c -> q t c", q=128)), writes=[pfb] + P1)
        em.op("dve", lambda e: e.tensor_copy(pb16, pf), reads=[pfb], writes=[pbb] + P1)
        for kc in range(2):
            pv = ps[kc].bitcast(BF16)[:, 0:512]
            for t in range(4):
                em.op("pe", lambda e, pv=pv, t=t, kc=kc: e.transpose(pv[:, t * 128:(t + 1) * 128], pb16[:, t, kc * 128:(kc + 1) * 128], self.cblk(0)), reads=[pbb, self.cb], writes=[psb[kc]])
            em.op("dve", lambda e, pv=pv, kc=kc: e.tensor_copy(pT[:, kc, :], pv), reads=[psb[kc]], writes=[pTb] + P1)
        for half in range(2):
            wg3, wgb = wload("w_ple_gate", half * 512, 512, 8)
            wp3, wpb = wload("w_ple_proj", half * 512, 512, 2)
            for t in range(4):
                bg, bp = 2 + (t % 2) * 2, 3 + (t % 2) * 2
                for kc in range(8):
                    em.op("pe", lambda e, wg3=wg3, kc=kc, t=t, bg=bg: e.matmul(ps[bg], hT2[:, kc, t * 128:(t + 1) * 128], wg3[:, kc, :], start=(kc == 0), stop=(kc == 7)), reads=[wgb, hT2b], writes=[psb[bg]])
                for kc in range(2):
                    em.op("pe", lambda e, wp3=wp3, kc=kc, t=t, bp=bp: e.matmul(ps[bp], pT[:, kc, t * 128:(t + 1) * 128], wp3[:, kc, :], start=(kc == 0), stop=(kc == 1)), reads=[wpb, pTb], writes=[psb[bp]])
                t_, tb_ = tmp()
                em.op("act", lambda e, t_=t_, bg=bg: e.activation(t_, ps[bg], AF.Sigmoid), reads=[psb[bg]], writes=[tb_])
                em.op("dve", lambda e, t_=t_, bp=bp: e.tensor_tensor(t_, t_, ps[bp], ALU.mult), reads=[tb_, psb[bp]], writes=[tb_])
                em.op("pool", lambda e, t_=t_, t=t, half=half: e.tensor_tensor(xblk[:, t, half * 512:(half + 1) * 512], xblk[:, t, half * 512:(half + 1) * 512], t_, ALU.add), reads=[tb_, xbb[t]], writes=[xbb[t]])
        if not last:
            for t in range(4):
                em.dma("pool", lambda e, t=t: e.dma_start(self.xs1[g0 + t * 128:g0 + (t + 1) * 128, :], xblk[:, t, :]), reads=[xbb[t]], writes=[self.db("xs1", gb)])
        else:
            yt = C["yt"]
            gfin = self.rpf[:, RP_FIN:RP_FIN + D]
            for t in range(4):
                i = self.n_i
                self.n_i += 1
                stt, stb = self.n_st[i % 2]
                y_, yb_ = yt[t % 2]
                jb = self.db("junk")
                em.op("act", lambda e, t=t, stt=stt: e.activation(self.n_junk, xblk[:, t, :], AF.Square, accum_out=stt[:, 0:1]), reads=[xbb[t]], writes=[jb, stb])
                em.op("dve", lambda e, stt=stt: e.tensor_scalar(stt[:, 1:2], stt[:, 0:1], 1.0 / D, EPS, ALU.mult, ALU.add), reads=[stb], writes=[stb])
                em.op("act", lambda e, stt=stt: e.activation(stt[:, 2:3], stt[:, 1:2], AF.Sqrt), reads=[stb], writes=[stb])
                em.op("dve", lambda e, stt=stt: e.reciprocal(stt[:, 3:4], stt[:, 2:3]), reads=[stb], writes=[stb])
                em.op("dve", lambda e, t=t, stt=stt, y_=y_: e.scalar_tensor_tensor(y_, xblk[:, t, :], stt[:, 3:4], gfin, ALU.mult, ALU.mult), reads=[xbb[t], stb, self.cb], writes=[yb_] + P1)
                em.dma("pool", lambda e, t=t, y_=y_: e.dma_start(self.y[g0 + t * 128:g0 + (t + 1) * 128, :], y_), reads=[yb_], writes=[self.db("y", gb, t)])

    def stage_Cprep(self, si, tok0, S, l, hT, hTb):
        em = self.em
        ps, psb = self.ps, self.psb
        NT, NB = S // 128, S // 512
        o = l * SP_PER_LAYER
        r_ = l * RP_PER_LAYER
        diag = self.alloc(60 * 128, BF16).rearrange("p (c j n) -> p c j n", c=12, j=5)
        diagb = self.nb("diag")
        for cc in range(12):
            for j in range(5):
                col = o + SP_CONVW + cc * 5 + j
                em.op("dve", lambda e, cc=cc, j=j, col=col: e.tensor_scalar(diag[:, cc, j, :], self.cblk(0, bf=False), self.spf[:, col:col + 1], None, ALU.mult), reads=[self.cb], writes=[diagb])
        xraw = self.alloc(S + 4, BF16)
        xrb = [self.nb("xraw") for _ in range(NB)]
        em.op("pool", lambda e: e.memset(xraw[:, 0:2], 0.0), writes=[xrb[0]])
        em.op("pool", lambda e: e.memset(xraw[:, S + 2:S + 4], 0.0), writes=[xrb[NB - 1]])
        wx = self.ring(2, 8 * 128, BF16, "wx")
        xc = self.ring(3, 512, BF16, "xc")
        xst = self.ring(2, 512, BF16, "xst")
        k = [0]

        def chunk(cc):
            wt, wtb = wx[cc % 2]
            w3 = wt.rearrange("p (k n) -> p k n", k=8)
            em.dma("sp", lambda e: e.dma_start(w3, self.wslab("w_in", l, OFF_XBC + cc * 128, 128, 8)), writes=[wtb])
            for blk in range(NB):
                bi = blk % 2
                for kc in range(8):
                    em.op("pe", lambda e, kc=kc, blk=blk, bi=bi: e.matmul(ps[bi], w3[:, kc, :], hT[:, kc, blk * 512:(blk + 1) * 512], start=(kc == 0), stop=(kc == 7)), reads=[wtb, hTb[blk]], writes=[psb[bi]])
                em.op("act", lambda e, blk=blk, bi=bi: e.copy(xraw[:, 2 + blk * 512:2 + (blk + 1) * 512], ps[bi]), reads=[psb[bi]], writes=[xrb[blk]])
            for blk in range(NB):
                bi = 2 + blk % 2
                g0 = tok0 + blk * 512
                for j in range(5):
                    em.op("pe", lambda e, j=j, blk=blk, bi=bi: e.matmul(ps[bi], diag[:, cc, j, :], xraw[:, blk * 512 + j:blk * 512 + j + 512], start=(j == 0), stop=(j == 4)),
                          reads=[diagb, xrb[max(0, blk - 1)], xrb[blk], xrb[min(NB - 1, blk + 1)]], writes=[psb[bi]])
                x_, xb_ = xc[k[0] % 3]
                k[0] += 1
                col = o + SP_CONVB + cc
                em.op("act", lambda e, x_=x_, bi=bi, col=col: e.activation(x_, ps[bi], AF.Silu, bias=self.spf[:, col:col + 1]), reads=[psb[bi], self.cb], writes=[xb_])
                if cc >= 8:
                    em.dma("pool", lambda e, x_=x_, g0=g0: e.dma_start(self.bct[(cc - 8) * 128:(cc - 7) * 128, g0:g0 + 512], x_), reads=[xb_], writes=[self.db("bct", cc, g0 // 512)])
                if cc < 10:
                    bt = 4 + blk % 2
                    pv = ps[bt].bitcast(BF16)[:, 0:512]
                    for t in range(4):
                        em.op("pe", lambda e, pv=pv, t=t, x_=x_, bt=bt: e.transpose(pv[:, t * 128:(t + 1) * 128], x_[:, t * 128:(t + 1) * 128], self.cblk(0)), reads=[xb_, self.cb], writes=[psb[bt]])
                    s_, sb_ = xst[k[0] % 2]
                    em.op("dve", lambda e, pv=pv, s_=s_: e.tensor_copy(s_, pv), reads=[psb[bt]], writes=[sb_])
                    if cc < 8:
                        dstv = self.xstok[g0:g0 + 512, cc * 128:(cc + 1) * 128].rearrange("(t q) c -> q t c", q=128)
                        key = self.db("xstok", cc, g0 // 512)
                    else:
                        dstv = self.btok[g0:g0 + 512, (cc - 8) * 128:(cc - 7) * 128].rearrange("(t q) c -> q t c", q=128)
                        key = self.db("btok", cc, g0 // 512)
                    em.dma("pool", lambda e, dstv=dstv, s_=s_: e.dma_start(dstv, s_.rearrange("p (t c) -> p t c", t=4)), reads=[sb_], writes=[key])
        for cc in range(12):
            chunk(cc)
        wz = self.ring(2, 8 * 512, BF16, "wz")
        szt = self.ring(3, 512, BF16, "szt")
        kk = 0
        for half in range(2):
            wt, wtb = wz[half]
            w3 = wt.rearrange("p (k n) -> p k n", k=8)
            em.dma("sp", lambda e, w3=w3, half=half: e.dma_start(w3, self.wslab("w_in", l, OFF_Z + half * 512, 512, 8)), writes=[wtb])
            for tt in range(NT):
                bi = kk % 4
                s_, sb_ = szt[kk % 3]
                kk += 1
                for kc in range(8):
                    em.op("pe", lambda e, w3=w3, kc=kc, tt=tt, bi=bi: e.matmul(ps[bi], hT[:, kc, tt * 128:(tt + 1) * 128], w3[:, kc, :], start=(kc == 0), stop=(kc == 7)), reads=[wtb, hTb[tt // 4]], writes=[psb[bi]])
                em.op("act", lambda e, s_=s_, bi=bi: e.activation(s_, ps[bi], AF.Silu), reads=[psb[bi]], writes=[sb_])
                t0 = tok0 + tt * 128
                em.dma("pool", lambda e, s_=s_, t0=t0, half=half: e.dma_start(self.szd[t0:t0 + 128, half * 512:(half + 1) * 512], s_), reads=[sb_], writes=[self.db("szd", t0 // 128, half)])
        wdt = self.alloc3(8, 32, BF16)
        wdtb = self.nb("wdt")
        em.dma("sp", lambda e: e.dma_start(wdt, self.wslab("w_in", l, OFF_DT, 32, 8)), writes=[wdtb])
        dtb_ = self.nb("dt")
        self.dtb_ = dtb_
        tmpd = self.alloc3(16, 32, F32)
        tmpb = self.nb("tmpd")
        for t16 in range(0, NT, 16):
            n = min(16, NT - t16)
            bi = 6 + (t16 // 16) % 2
            for ti in range(n):
                tt = t16 + ti
                for kc in range(8):
                    em.op("pe", lambda e, kc=kc, tt=tt, ti=ti, bi=bi: e.matmul(ps[bi][:, ti * 32:(ti + 1) * 32], hT[:, kc, tt * 128:(tt + 1) * 128], wdt[:, kc, :], start=(kc == 0), stop=(kc == 7)), reads=[wdtb, hTb[tt // 4]], writes=[psb[bi]])
            bias = self.rpf[:, r_ + RP_DTB:r_ + RP_DTB + 32].unsqueeze(1).broadcast_to([128, n, 32])
            em.op("dve", lambda e, n=n, bi=bi, bias=bias: e.tensor_tensor(tmpd[:, 0:n, :], ps[bi][:, 0:n * 32].rearrange("p (t c) -> p t c", c=32), bias, ALU.add), reads=[psb[bi], self.cb], writes=[tmpb])
            em.op("act", lambda e, n=n: e.activation(tmpd[:, 0:n, :], tmpd[:, 0:n, :], AF.Exp), reads=[tmpb], writes=[tmpb])
            em.op("act", lambda e, n=n, t16=t16: e.activation(self.dt_[:, t16:t16 + n, :], tmpd[:, 0:n, :], AF.Ln, bias=1.0), reads=[tmpb], writes=[dtb_])

    def stage_Cscan(self, si, tok0, S, l):
        em = self.em
        ps, psb = self.ps, self.psb
        NT = S // 128
        o = l * SP_PER_LAYER
        r_ = l * RP_PER_LAYER
        dt = self.dt_
        dtb_ = self.dtb_
        sel = self.alloc(32 * 128, BF16)[0:32, :]
        selb = self.nb("sel")
        em.dma("pool", lambda e: e.dma_start(sel, self.sel32d[:, :]), writes=[selb])
        adt = self.alloc3(NT, 32, F32)
        cs = self.alloc3(NT, 32, F32)
        ecs = self.alloc3(NT, 32, F32)
        w2 = self.alloc3(NT, 32, F32)
        cd = self.alloc3(NT, 32, F32)
        csT = self.alloc(NT * 128, F32)[0:32, :]
        smb = self.nb("small")
        csTb = self.nb("csT")
        csH = self.alloc(NT * 128, BF16)[0:32, :]
        csL = self.alloc(NT * 128, BF16)[0:32, :]
        csR = self.alloc(NT * 128, F32)[0:32, :]
        an = self.aneg[:, l * 32:(l + 1) * 32].unsqueeze(1).broadcast_to([128, NT, 32])
        em.op("dve", lambda e: e.tensor_tensor(adt, dt, an, ALU.mult), reads=[dtb_, self.cb], writes=[smb])
        tri_le, tri_ge, sel_last, sel_first, ident_f = (self.cblk(i, bf=False) for i in (6, 7, 8, 9, 0))
        for t16 in range(0, NT, 16):
            n = min(16, NT - t16)
            bi = (t16 // 16) % 2
            for ti in range(n):
                tt = t16 + ti
                em.op("pe", lambda e, tt=tt, ti=ti, bi=bi: e.matmul(ps[bi][:, ti * 32:ti * 32 + 16], tri_le, adt[:, tt, 0:16], start=True, stop=True), reads=[smb, self.cb], writes=[psb[bi]])
                em.op("pe", lambda e, tt=tt, ti=ti, bi=bi: e.matmul(ps[bi][:, ti * 32 + 16:ti * 32 + 32], tri_ge, adt[:, tt, 16:32], start=True, stop=True), reads=[smb, self.cb], writes=[psb[bi]])
            em.op("dve", lambda e, n=n, bi=bi, t16=t16: e.tensor_copy(cs[:, t16:t16 + n, :], ps[bi][:, 0:n * 32].rearrange("p (t c) -> p t c", c=32)), reads=[psb[bi]], writes=[smb])
            b2 = 2 + (t16 // 16) % 2
            for ti in range(n):
                tt = t16 + ti
                em.op("pe", lambda e, tt=tt, ti=ti, b2=b2: e.matmul(ps[b2][:, ti * 32:ti * 32 + 16], sel_last, cs[:, tt, 0:16], start=True, stop=True), reads=[smb, self.cb], writes=[psb[b2]])
                em.op("pe", lambda e, tt=tt, ti=ti, b2=b2: e.matmul(ps[b2][:, ti * 32 + 16:ti * 32 + 32], sel_first, cs[:, tt, 16:32], start=True, stop=True), reads=[smb, self.cb], writes=[psb[b2]])
            lv = lambda b2=b2, n=n: ps[b2][:, 0:n * 32].rearrange("p (t c) -> p t c", c=32)
            em.op("dve", lambda e, n=n, t16=t16, lv=lv: e.tensor_tensor(w2[:, t16:t16 + n, :], lv(), cs[:, t16:t16 + n, :], ALU.subtract), reads=[psb[b2], smb], writes=[smb])
            em.op("act", lambda e, n=n, t16=t16: e.activation(w2[:, t16:t16 + n, :], w2[:, t16:t16 + n, :], AF.Exp), reads=[smb], writes=[smb])
            em.op("act", lambda e, n=n, t16=t16, lv=lv: e.activation(cd[:, t16:t16 + n, :], lv(), AF.Exp), reads=[psb[b2]], writes=[smb])
        em.op("dve", lambda e: e.tensor_tensor(w2, w2, dt, ALU.mult), reads=[smb, dtb_], writes=[smb])
        em.op("act", lambda e: e.activation(ecs, cs, AF.Exp), reads=[smb], writes=[smb])
        for t4 in range(0, NT, 4):
            bi = 4 + (t4 // 4) % 2
            for ti in range(4):
                tt = t4 + ti
                em.op("pe", lambda e, tt=tt, ti=ti, bi=bi: e.transpose(ps[bi][0:32, ti * 128:(ti + 1) * 128], cs[:, tt, :], ident_f), reads=[smb, self.cb], writes=[psb[bi]])
            em.op("dve", lambda e, t4=t4, bi=bi: e.tensor_copy(csT[:, t4 * 128:(t4 + 4) * 128], ps[bi][0:32, :]), reads=[psb[bi]], writes=[csTb])
        em.op("dve", lambda e: e.tensor_copy(csH, csT), reads=[csTb], writes=[csTb])
        em.op("dve", lambda e: e.tensor_tensor(csR, csT, csH, ALU.subtract), reads=[csTb], writes=[csTb])
        em.op("dve", lambda e: e.tensor_copy(csL, csR), reads=[csTb], writes=[csTb])
        dsk = self.rpf[:, r_ + RP_SSMD:r_ + RP_SSMD + 16]
        H = self.alloc(512, F32)
        Hb = self.nb("H")
        Hbf = self.ring(2, 512, BF16, "Hbf")
        xs_r = self.ring(2, 512, BF16, "xs")
        bt_r = self.ring(2, 128, BF16, "btok")
        BT_r = self.ring(2, 128, BF16, "BT")
        CT_r = self.ring(2, 128, BF16, "CT")
        hb_r = self.ring(2, 512, BF16, "hbl")
        sz_r = self.ring(2, 512, BF16, "sz")
        Xd_r = self.ring(3, 512, BF16, "Xd")
        Gs_r = self.ring(2, 128, BF16, "Gs")
        Dm_r = self.ring(8, 128, F32, "Dm")
        Lm_r = self.ring(8, 128, BF16, "Lm")
        Mm_r = self.ring(8, 128, BF16, "Mm")
        tf2 = [self.ring(6, 512, F32, "tf") for _ in range(2)]
        yn_r = self.ring(2, 512, BF16, "yn")
        oc_r = self.ring(2, 4 * 512, BF16, "ocT")
        st_r = self.ring(2, 4, F32, "cst")
        negm = (self.cblk(10, bf=False), self.cblk(11, bf=False))
        ctr = {"x": 0, "m": 0, "t": 0}

        def loads(gi, c, full):
            t0 = tok0 + c * 128
            i = ctr["x"]
            ctr["x"] += 1
            xs_, xsb = xs_r[i % 2]
            bt_, btb = bt_r[i % 2]
            em.dma("sp", lambda e: e.dma_start(xs_, self.xstok[t0:t0 + 128, gi * 512:(gi + 1) * 512]), reads=[self.db("xstok", gi * 4 + j, t0 // 512) for j in range(4)], writes=[xsb])
            em.dma("sp", lambda e: e.dma_start(bt_, self.btok[t0:t0 + 128, gi * 128:(gi + 1) * 128]), reads=[self.db("btok", 8 + gi, t0 // 512)], writes=[btb])
            if not full:
                return (xs_, xsb), (bt_, btb)
            BT_, BTb = BT_r[i % 2]
            CT_, CTb = CT_r[i % 2]
            hb_, hbb = hb_r[i % 2]
            sz_, szb = sz_r[i % 2]
            em.dma("sp", lambda e: e.dma_start(BT_, self.bct[gi * 128:(gi + 1) * 128, t0:t0 + 128]), reads=[self.db("bct", 8 + gi, t0 // 512)], writes=[BTb])
            em.dma("sp", lambda e: e.dma_start(CT_, self.bct[(2 + gi) * 128:(3 + gi) * 128, t0:t0 + 128]), reads=[self.db("bct", 10 + gi, t0 // 512)], writes=[CTb])
            em.dma("sp", lambda e: e.dma_start(hb_, self.hbd[gi, t0 // 128, :, :]), reads=[self.db("hbd", gi, t0 // 128)], writes=[hbb])
            em.dma("sp", lambda e: e.dma_start(sz_, self.szd[t0:t0 + 128, gi * 512:(gi + 1) * 512]), reads=[self.db("szd", t0 // 128, gi)], writes=[szb])
            return (xs_, xsb), (bt_, btb), (BT_, BTb), (CT_, CTb), (hb_, hbb), (sz_, szb)

        def bc8(arr, c, col0):
            return arr[:, c, col0:col0 + 8].unsqueeze(2).broadcast_to([128, 8, 64])

        def v3(a):
            return a.rearrange("p (e d) -> p e d", e=8)

        def state_update(gi, c, d, xs_, xsb, bt_, btb):
            col0 = d * 16 + gi * 8
            Xd, Xdb = Xd_r[ctr["m"] % 3]
            ctr["m"] += 1
            em.op("dve", lambda e: e.tensor_tensor(v3(Xd), v3(xs_), bc8(w2, c, col0), ALU.mult), reads=[xsb, smb], writes=[Xdb])
            em.op("pe", lambda e: e.matmul(ps[6], bt_, Xd, start=True, stop=True), reads=[btb, Xdb], writes=[psb[6]])
            em.op("dve", lambda e: e.tensor_tensor(v3(H), v3(H), bc8(cd, c, col0), ALU.mult), reads=[Hb, smb], writes=[Hb])
            em.op("dve", lambda e: e.tensor_tensor(H, H, ps[6], ALU.add), reads=[Hb, psb[6]], writes=[Hb])

        def bwd_chunk(gi, c):
            (xs_, xsb), (bt_, btb) = loads(gi, c, False)
            hbf, hbfb = Hbf[c % 2]
            em.op("act", lambda e: e.copy(hbf, H), reads=[Hb], writes=[hbfb])
            em.dma("pool", lambda e: e.dma_start(self.hbd[gi, (tok0 // 128) + c, :, :], hbf), reads=[hbfb], writes=[self.db("hbd", gi, (tok0 // 128) + c)])
            state_update(gi, c, 1, xs_, xsb, bt_, btb)

        def fwd_chunk(gi, c):
            (xs_, xsb), (bt_, btb), (BT_, BTb), (CT_, CTb), (hb_, hbb), (sz_, szb) = loads(gi, c, True)
            hbf, hbfb = Hbf[c % 2]
            em.op("act", lambda e: e.copy(hbf, H), reads=[Hb], writes=[hbfb])
            Gs, Gsb = Gs_r[c % 2]
            em.op("pe", lambda e: e.matmul(ps[0][:, 0:128], BT_, CT_, start=True, stop=True), reads=[BTb, CTb], writes=[psb[0]])
            em.op("act", lambda e: e.copy(Gs, ps[0][:, 0:128]), reads=[psb[0]], writes=[Gsb])
            X = []
            for d in range(2):
                Xd, Xdb = Xd_r[ctr["m"] % 3]
                ctr["m"] += 1
                em.op("dve", lambda e, Xd=Xd, d=d: e.tensor_tensor(v3(Xd), v3(xs_), bc8(dt, c, d * 16 + gi * 8), ALU.mult), reads=[xsb, dtb_], writes=[Xdb])
                X.append((Xd, Xdb))
            for rnd in range(2):
                combos = [(e_, d) for e_ in range(rnd * 4, rnd * 4 + 4) for d in range(2)]
                for i, (e_, d) in enumerate(combos):
                    col = d * 16 + gi * 8 + e_
                    rbv = ps[1 + i // 4][:, (i % 4) * 128:(i % 4 + 1) * 128]
                    em.op("pe", lambda e, rbv=rbv, col=col: e.matmul(rbv, sel[:, col * 128:(col + 1) * 128], csH[:, c * 128:(c + 1) * 128], start=True, stop=False), reads=[selb, csTb], writes=[psb[1 + i // 4]])
                    em.op("pe", lambda e, rbv=rbv, col=col: e.matmul(rbv, sel[:, col * 128:(col + 1) * 128], csL[:, c * 128:(c + 1) * 128], start=False, stop=True), reads=[selb, csTb], writes=[psb[1 + i // 4]])
                for i, (e_, d) in enumerate(combos):
                    col = d * 16 + gi * 8 + e_
                    rbv = ps[1 + i // 4][:, (i % 4) * 128:(i % 4 + 1) * 128]
                    Dm, Dmb = Dm_r[i]
                    Lm, Lmb = Lm_r[i]
                    Mm, Mmb = Mm_r[i]
                    em.op("dve", lambda e, rbv=rbv, col=col, Dm=Dm, d=d: e.scalar_tensor_tensor(Dm, rbv, cs[:, c, col:col + 1], negm[d], ALU.subtract, ALU.add), reads=[psb[1 + i // 4], smb, self.cb], writes=[Dmb])
                    em.op("act", lambda e, Dm=Dm, Lm=Lm: e.activation(Lm, Dm, AF.Exp), reads=[Dmb], writes=[Lmb])
                    em.op("pool", lambda e, Lm=Lm, Mm=Mm: e.tensor_tensor(Mm, Lm, Gs, ALU.mult), reads=[Lmb, Gsb], writes=[Mmb])
                for i, (e_, d) in enumerate(combos):
                    Mm, Mmb = Mm_r[i]
                    Xd, Xdb = X[d]
                    em.op("pe", lambda e, Mm=Mm, Xd=Xd, e_=e_, d=d: e.matmul(ps[3][:, e_ * 64:(e_ + 1) * 64], Mm, Xd[:, e_ * 64:(e_ + 1) * 64], start=(d == 0), stop=(d == 1)), reads=[Mmb, Xdb], writes=[psb[3]])
            em.op("pe", lambda e: e.matmul(ps[4], CT_, hbf, start=True, stop=True), reads=[CTb, hbfb], writes=[psb[4]])
            em.op("pe", lambda e: e.matmul(ps[5], CT_, hb_, start=True, stop=True), reads=[CTb, hbb], writes=[psb[5]])
            (t1, t1b), (t2, t2b), (t3, t3b), (s1, s1b), (s2, s2b), (yz, yzb) = tf2[c % 2]
            em.op("dve", lambda e: e.tensor_tensor(v3(t1), v3(ps[4]), bc8(ecs, c, gi * 8), ALU.mult), reads=[psb[4], smb], writes=[t1b])
            em.op("dve", lambda e: e.tensor_tensor(v3(t2), v3(ps[5]), bc8(ecs, c, 16 + gi * 8), ALU.mult), reads=[psb[5], smb], writes=[t2b])
            dk = dsk[:, gi * 8:gi * 8 + 8].unsqueeze(2).broadcast_to([128, 8, 64])
            em.op("pool", lambda e: e.tensor_tensor(v3(t3), v3(xs_), dk, ALU.mult), reads=[xsb, self.cb], writes=[t3b])
            em.op("dve", lambda e: e.tensor_tensor(s1, ps[3], t1, ALU.add), reads=[psb[3], t1b], writes=[s1b])
            em.op("pool", lambda e: e.tensor_tensor(s2, t2, t3, ALU.add), reads=[t2b, t3b], writes=[s2b])
            em.op("pool", lambda e: e.tensor_tensor(s1, s1, s2, ALU.add), reads=[s1b, s2b], writes=[s1b])
            em.op("dve", lambda e: e.tensor_tensor(yz, s1, sz_, ALU.mult), reads=[s1b, szb], writes=[yzb])
            stt, stb = st_r[c % 2]
            jb = self.db("junk")
            em.op("act", lambda e: e.activation(self.n_junk[:, 0:512], yz, AF.Square, accum_out=stt[:, 0:1]), reads=[yzb], writes=[jb, stb])
            em.op("dve", lambda e: e.tensor_scalar(stt[:, 1:2], stt[:, 0:1], 1.0 / 512, EPS, ALU.mult, ALU.add), reads=[stb], writes=[stb])
            em.op("act", lambda e: e.activation(stt[:, 2:3], stt[:, 1:2], AF.Sqrt), reads=[stb], writes=[stb])
            em.op("dve", lambda e: e.reciprocal(stt[:, 3:4], stt[:, 2:3]), reads=[stb], writes=[stb])
            yn, ynb = yn_r[c % 2]
            em.op("dve", lambda e: e.tensor_scalar(yn, yz, stt[:, 3:4], None, ALU.mult), reads=[yzb, stb], writes=[ynb])
            oc, ocb = oc_r[(c // 4) % 2]
            oc3 = oc.rearrange("p (j t) -> p j t", j=4)
            pv = ps[7].bitcast(BF16)[:, 0:512]
            for j in range(4):
                em.op("pe", lambda e, j=j: e.transpose(pv[:, j * 128:(j + 1) * 128], yn[:, j * 128:(j + 1) * 128], self.cblk(0)), reads=[ynb, self.cb], writes=[psb[7]])
            g4 = self.spf[:, o + SP_GSSM + gi * 4:o + SP_GSSM + gi * 4 + 4].unsqueeze(2).broadcast_to([128, 4, 128])
            tl = c % 4
            em.op("dve", lambda e: e.tensor_tensor(oc3[:, :, tl * 128:(tl + 1) * 128], pv.rearrange("p (j n) -> p j n", j=4), g4, ALU.mult), reads=[psb[7], self.cb], writes=[ocb])
            if tl == 3:
                g0 = tok0 + (c - 3) * 128
                em.dma("pool", lambda e: e.dma_start(self.oT[(8 + gi * 4) * 128:(12 + gi * 4) * 128, g0:g0 + 512].rearrange("(j p) t -> p j t", p=128), oc3), reads=[ocb], writes=[self.db("oT", g0 // 512, "c%d" % gi)])
            state_update(gi, c, 0, xs_, xsb, bt_, btb)

        for gi in range(2):
            em.op("pool", lambda e: e.memset(H, 0.0), writes=[Hb])
            for c in range(NT - 1, -1, -1):
                bwd_chunk(gi, c)
            em.op("pool", lambda e: e.memset(H, 0.0), writes=[Hb])
            for c in range(NT):
                fwd_chunk(gi, c)


SEQS = (2048, 2048, 4096, 4096)
N_CORES = 8


def kernel(**inputs):
    inp = {k: np.asarray(v) for k, v in inputs.items()}
    xp, xs = inp["x_prompt"], inp["x_sample"]
    pp, psm = inp["p_prompt"], inp["p_sample"]
    cpack, cosT, sinT, sel32 = make_consts()
    sp, rp = make_packs(inp)
    b = Builder(list(SEQS))
    nc = b.build()
    in_maps = []
    for i in range(N_CORES):
        x = np.concatenate([xp[2 * i], xp[2 * i + 1], xs[2 * i], xs[2 * i + 1]], axis=0)
        p = np.concatenate([pp[:, 2 * i], pp[:, 2 * i + 1], psm[:, 2 * i], psm[:, 2 * i + 1]], axis=1)
        m = {"x": np.ascontiguousarray(x, dtype=np.float32), "p": np.ascontiguousarray(p, dtype=np.float32),
             "cpack": cpack, "cosT": cosT, "sinT": sinT, "sel32": sel32, "spack": sp, "rpack": rp}
        for name, k, n in WEIGHTS:
            m[name] = np.ascontiguousarray(inp[name], dtype=np.float32)
        in_maps.append(m)
    res = run_bass_kernel_spmd(nc, in_maps, core_ids=list(range(N_CORES)))
    y_prompt = np.empty(xp.shape, np.float32)
    y_sample = np.empty(xs.shape, np.float32)
    for i in range(N_CORES):
        y = np.asarray(res.results[i]["y"], dtype=np.float32)
        y_prompt[2 * i] = y[0:2048]
        y_prompt[2 * i + 1] = y[2048:4096]
        y_sample[2 * i] = y[4096:8192]
        y_sample[2 * i + 1] = y[8192:12288]
    return (y_prompt, y_sample)
```
